# Optimizing a Trainium2 kernel written in Bass

```python
import math
import jax, jax.numpy as jnp
from jax import lax
import numpy as np

D_MODEL = 2048
BATCH = 8
SEQ = 2048
DEPTH = 1

GRID_W = 64
CTX_LEN = 256
HEAD_DIM = 128
N_HEADS = D_MODEL // HEAD_DIM
N_KV_HEADS = N_HEADS // 4
WINDOW = 128
BLOCK = 128
C_CONV = D_MODEL
CONV_K = 31
FFN_HIDDEN = -(-8 * D_MODEL // (3 * 256)) * 256
ROPE_BASE = 10000.0
ALPHA = (2.0 * DEPTH) ** 0.25
BETA = (8.0 * DEPTH) ** -0.25
EPS = 1e-6
NEG_INF = -1e30

Q_COLS = N_HEADS * HEAD_DIM
KV_COLS = N_KV_HEADS * HEAD_DIM
OFF_K = Q_COLS
OFF_V = OFF_K + KV_COLS
OFF_GLU = OFF_V + KV_COLS
OFF_GATE = OFF_GLU + 2 * C_CONV
IN_COLS = OFF_GATE + 2 * D_MODEL

kernel_name = 'hybrid_gqa_conformer_dit_layer'


def layer_norm(x, g=None, b=None):
    xf = x.astype(jnp.float32)
    mu = jnp.mean(xf, axis=-1, keepdims=True)
    var = jnp.mean(jnp.square(xf - mu), axis=-1, keepdims=True)
    y = (xf - mu) * lax.rsqrt(var + EPS)
    if g is not None:
        y = y * g.astype(jnp.float32) + b.astype(jnp.float32)
    return y.astype(x.dtype)


def modulate(x, shift, scale):
    return x * (1.0 + scale) + shift


def heads(t, n):
    return t.reshape(t.shape[:-1] + (n, HEAD_DIM))


def axial_rope_tables(n_tokens):
    rows = n_tokens // GRID_W
    row = jnp.repeat(jnp.arange(rows, dtype=jnp.int32), GRID_W)
    col = jnp.tile(jnp.arange(GRID_W, dtype=jnp.int32), rows)
    n_freq = HEAD_DIM // 4
    inv_freq = ROPE_BASE ** (-jnp.arange(n_freq, dtype=jnp.float32) / n_freq)
    ang = jnp.stack([row.astype(jnp.float32)[:, None] * inv_freq,
                     col.astype(jnp.float32)[:, None] * inv_freq], axis=1)
    return jnp.cos(ang), jnp.sin(ang)


def apply_axial_rope(x, cos, sin):
    B, S, H, _ = x.shape
    xr = x.astype(jnp.float32).reshape(B, S, H, 2, 2, HEAD_DIM // 4)
    x1, x2 = xr[..., 0, :], xr[..., 1, :]
    c = cos[None, :, None]
    s = sin[None, :, None]
    out = jnp.stack([x1 * c - x2 * s, x1 * s + x2 * c], axis=-2)
    return out.reshape(B, S, H, HEAD_DIM).astype(x.dtype)


def latent_window_attention(q, k, v, k_ctx, v_ctx, sink):
    B, S, H, dh = q.shape
    nb = S // BLOCK
    G = H // N_KV_HEADS
    scale = HEAD_DIM ** -0.5
    qb = jnp.moveaxis(q.reshape(B, nb, BLOCK, N_KV_HEADS, G, dh), 1, 0)
    pad = ((0, 0), (BLOCK, BLOCK), (0, 0), (0, 0))

    def band(t):
        tb = jnp.pad(t, pad).reshape(B, nb + 2, BLOCK, N_KV_HEADS, dh)
        tb = jnp.concatenate([tb[:, :-2], tb[:, 1:-1], tb[:, 2:]], axis=2)
        return jnp.moveaxis(tb, 1, 0)

    kb, vb = band(k), band(v)
    sink_l = sink.astype(jnp.float32).reshape(N_KV_HEADS, G)[None, :, :, None, None]
    q_off = jnp.arange(BLOCK, dtype=jnp.int32)
    k_off = jnp.arange(3 * BLOCK, dtype=jnp.int32)

    def per_block(args):
        qblk, kblk, vblk, bidx = args
        i_abs = bidx * BLOCK + q_off
        j_abs = bidx * BLOCK - BLOCK + k_off
        valid = (jnp.abs(j_abs[None, :] - i_abs[:, None]) <= WINDOW) & (j_abs >= 0)[None, :] & (j_abs < S)[None, :]
        s_win = jnp.einsum('bqkgd,bskd->bkgqs', qblk, kblk, preferred_element_type=jnp.float32) * scale
        s_win = jnp.where(valid, s_win, NEG_INF)
        s_ctx = jnp.einsum('bqkgd,blkd->bkgql', qblk, k_ctx, preferred_element_type=jnp.float32) * scale
        s_sink = jnp.broadcast_to(sink_l, s_win.shape[:-1] + (1,))
        p = jax.nn.softmax(jnp.concatenate([s_win, s_ctx, s_sink], axis=-1), axis=-1)
        p_win = p[..., :3 * BLOCK].astype(vblk.dtype)
        p_ctx = p[..., 3 * BLOCK:3 * BLOCK + k_ctx.shape[1]].astype(v_ctx.dtype)
        return (jnp.einsum('bkgqs,bskd->bqkgd', p_win, vblk)
                + jnp.einsum('bkgql,blkd->bqkgd', p_ctx, v_ctx))

    out = lax.map(per_block, (qb, kb, vb, jnp.arange(nb, dtype=jnp.int32)))
    return jnp.moveaxis(out, 0, 1).reshape(B, S, H * dh)


def context_attention(q, k, v, sink):
    B, L, H, dh = q.shape
    G = H // N_KV_HEADS
    qg = q.reshape(B, L, N_KV_HEADS, G, dh)
    s = jnp.einsum('bqkgd,bskd->bkgqs', qg, k, preferred_element_type=jnp.float32) * (HEAD_DIM ** -0.5)
    s_sink = jnp.broadcast_to(sink.astype(jnp.float32).reshape(N_KV_HEADS, G)[None, :, :, None, None], s.shape[:-1] + (1,))
    p = jax.nn.softmax(jnp.concatenate([s, s_sink], axis=-1), axis=-1)[..., :-1]
    return jnp.einsum('bkgqs,bskd->bqkgd', p.astype(v.dtype), v).reshape(B, L, H * dh)


def conformer_conv(glu_in, conv_w, conv_b, norm_g, norm_b, w_proj):
    a, gt = jnp.split(glu_in, 2, axis=-1)
    u = a * jax.nn.sigmoid(gt)
    y = lax.conv_general_dilated(u, conv_w[:, None, :], window_strides=(1,),
                                 padding=[(CONV_K // 2, CONV_K // 2)],
                                 dimension_numbers=('NWC', 'WIO', 'NWC'),
                                 feature_group_count=C_CONV) + conv_b
    y = jax.nn.silu(layer_norm(y, norm_g, norm_b))
    return y @ w_proj


def merge_branches(attn_flat, glu_in, gate_logits, lp):
    y_attn = attn_flat @ lp['w_attn_proj']
    y_conv = conformer_conv(glu_in, lp['conv_w'], lp['conv_b'], lp['conv_norm_g'], lp['conv_norm_b'], lp['w_conv_proj'])
    g_attn, g_conv = jnp.split(jax.nn.sigmoid(gate_logits), 2, axis=-1)
    return (g_attn * y_attn + g_conv * y_conv) @ lp['w_out']


def swiglu(h, w_in, w_out):
    g, u = jnp.split(h @ w_in, 2, axis=-1)
    return (jax.nn.silu(g) * u) @ w_out


def hybrid_layer(x, xc, mod, mod_c, lp, cos, sin, update_ctx):
    sh1, sc1, g1, sh2, sc2, g2 = jnp.split(mod, 6, axis=-1)
    csh1, csc1, cg1, csh2, csc2, cg2 = jnp.split(mod_c, 6, axis=-1)
    w_in = lp['w_in']
    h = modulate(x, sh1, sc1)
    hc = modulate(xc, csh1, csc1)
    kv_c = hc @ w_in[:, OFF_K:OFF_GLU]
    k_c = heads(kv_c[..., :KV_COLS], N_KV_HEADS)
    v_c = heads(kv_c[..., KV_COLS:], N_KV_HEADS)
    proj = h @ w_in
    q = apply_axial_rope(heads(proj[..., :OFF_K], N_HEADS), cos, sin)
    k = apply_axial_rope(heads(proj[..., OFF_K:OFF_V], N_KV_HEADS), cos, sin)
    v = heads(proj[..., OFF_V:OFF_GLU], N_KV_HEADS)
    attn = latent_window_attention(q, k, v, k_c, v_c, lp['attn_sink'])
    mix = merge_branches(attn, proj[..., OFF_GLU:OFF_GATE], proj[..., OFF_GATE:], lp)
    x_new = layer_norm(ALPHA * x + g1 * mix, lp['ln1_g'], lp['ln1_b'])
    ffn = swiglu(modulate(x_new, sh2, sc2), lp['w_ffn_in'], lp['w_ffn_out'])
    x_new = layer_norm(ALPHA * x_new + g2 * ffn, lp['ln2_g'], lp['ln2_b'])
    if update_ctx:
        q_c = heads(hc @ w_in[:, :OFF_K], N_HEADS)
        rest_c = hc @ w_in[:, OFF_GLU:]
        attn_c = context_attention(q_c, k_c, v_c, lp['attn_sink'])
        mix_c = merge_branches(attn_c, rest_c[..., :2 * C_CONV], rest_c[..., 2 * C_CONV:], lp)
        xc = layer_norm(ALPHA * xc + cg1 * mix_c, lp['ln1_g'], lp['ln1_b'])
        ffn_c = swiglu(modulate(xc, csh2, csc2), lp['w_ffn_in'], lp['w_ffn_out'])
        xc = layer_norm(ALPHA * xc + cg2 * ffn_c, lp['ln2_g'], lp['ln2_b'])
    return x_new, xc


def setup_inputs(seed: int = 0) -> dict:
    key = jax.random.key(seed)
    ks = jax.random.split(key, 24)
    f32 = jnp.float32

    def nrm(k, shape, scale):
        return jax.random.normal(k, shape, f32) * scale

    col_scale = jnp.ones((IN_COLS,), f32).at[OFF_V:OFF_GLU].set(BETA)
    return {
        'x': nrm(ks[0], (BATCH, SEQ, D_MODEL), 1.0),
        'c': nrm(ks[1], (BATCH, D_MODEL), 1.0),
        'ctx': nrm(ks[2], (BATCH, CTX_LEN, D_MODEL), 1.0),
        'c_ctx': nrm(ks[3], (D_MODEL,), 1.0),
        'w_mod': nrm(ks[4], (DEPTH, D_MODEL, 6 * D_MODEL), 0.5 * D_MODEL ** -0.5),
        'b_mod': nrm(ks[5], (DEPTH, 6 * D_MODEL), 0.02),
        'w_in': nrm(ks[6], (DEPTH, D_MODEL, IN_COLS), D_MODEL ** -0.5) * col_scale,
        'attn_sink': nrm(ks[7], (DEPTH, N_HEADS), 0.5),
        'conv_w': nrm(ks[8], (DEPTH, CONV_K, C_CONV), CONV_K ** -0.5),
        'conv_b': nrm(ks[9], (DEPTH, C_CONV), 0.02),
        'conv_norm_g': 1.0 + nrm(ks[10], (DEPTH, C_CONV), 0.02),
        'conv_norm_b': nrm(ks[11], (DEPTH, C_CONV), 0.02),
        'w_attn_proj': nrm(ks[12], (DEPTH, Q_COLS, D_MODEL), BETA * Q_COLS ** -0.5),
        'w_conv_proj': nrm(ks[13], (DEPTH, C_CONV, D_MODEL), BETA * C_CONV ** -0.5),
        'w_out': nrm(ks[14], (DEPTH, D_MODEL, D_MODEL), BETA * D_MODEL ** -0.5),
        'ln1_g': 1.0 + nrm(ks[15], (DEPTH, D_MODEL), 0.02),
        'ln1_b': nrm(ks[16], (DEPTH, D_MODEL), 0.02),
        'w_ffn_in': nrm(ks[17], (DEPTH, D_MODEL, 2 * FFN_HIDDEN), BETA * D_MODEL ** -0.5),
        'w_ffn_out': nrm(ks[18], (DEPTH, FFN_HIDDEN, D_MODEL), BETA * FFN_HIDDEN ** -0.5),
        'ln2_g': 1.0 + nrm(ks[19], (DEPTH, D_MODEL), 0.02),
        'ln2_b': nrm(ks[20], (DEPTH, D_MODEL), 0.02),
    }


def reference(x, c, ctx, c_ctx, w_mod, b_mod, w_in, attn_sink, conv_w, conv_b, conv_norm_g, conv_norm_b,
              w_attn_proj, w_conv_proj, w_out, ln1_g, ln1_b, w_ffn_in, w_ffn_out, ln2_g, ln2_b):
    cos, sin = axial_rope_tables(x.shape[1])
    x = layer_norm(x)
    xc = layer_norm(ctx)
    silu_c = jax.nn.silu(c)
    silu_cc = jax.nn.silu(c_ctx)
    for i in range(DEPTH):
        mod = (silu_c @ w_mod[i] + b_mod[i])[:, None, :]
        mod_c = silu_cc @ w_mod[i] + b_mod[i]
        lp = {
            'w_in': w_in[i], 'attn_sink': attn_sink[i],
            'conv_w': conv_w[i], 'conv_b': conv_b[i],
            'conv_norm_g': conv_norm_g[i], 'conv_norm_b': conv_norm_b[i],
            'w_attn_proj': w_attn_proj[i], 'w_conv_proj': w_conv_proj[i], 'w_out': w_out[i],
            'ln1_g': ln1_g[i], 'ln1_b': ln1_b[i],
            'w_ffn_in': w_ffn_in[i], 'w_ffn_out': w_ffn_out[i],
            'ln2_g': ln2_g[i], 'ln2_b': ln2_b[i],
        }
        x, xc = hybrid_layer(x, xc, mod, mod_c, lp, cos, sin, update_ctx=(i < DEPTH - 1))
    return x
```

```python
import os
import numpy as np
from contextlib import ExitStack
import concourse.bass as bass
import concourse.mybir as mybir
from concourse.bass_utils import run_bass_kernel_spmd

F32 = mybir.dt.float32
BF16 = mybir.dt.bfloat16
AF = mybir.ActivationFunctionType
ALU = mybir.AluOpType

D = 2048
S = 2048
L = 256
NH = 16
NKV = 4
HID = 5632
OFF_K, OFF_V, OFF_GLU, OFF_GATE, IN_COLS = 2048, 2560, 3072, 7168, 11264
ALPHA = 2.0 ** 0.25
EPS = 1e-6
HALF = 1024
EXT = 1152
NBLK = 9
UW = EXT + 30
SCALE = 128.0 ** -0.5

ENGS = ("pe", "act", "dve", "pool", "sp")
NDMASEM = 24
NHWSEM = 16


class WKey(tuple):
    gen = 0
    slots = ()


class Prog:
    def __init__(self):
        self.ops = {e: [] for e in ENGS}
        self.cnt = {e: 0 for e in ENGS}
        self.known = {e: {} for e in ENGS}
        self.buf = {}
        self.dma_uses = [0] * NDMASEM
        self.dma_rr = {"hw": 0, "sw": 0}
        self.pending = {e: False for e in ENGS}
        self.wgen = {}

    def _collect(self, eng, reads, writes):
        need = {}

        def add(tok):
            if tok is not None and need.get(tok[0], 0) < tok[1]:
                need[tok[0]] = tok[1]

        for r in reads:
            b = self.buf.get(r)
            if b:
                add(b[0])
        for w in writes:
            b = self.buf.get(w)
            if b:
                add(b[0])
                for t in b[1]:
                    add(t)
        out = []
        kn = self.known[eng]
        for s, v in need.items():
            if eng == "pe" and s == "pe":
                continue
            if kn.get(s, 0) >= v:
                continue
            kn[s] = v
            out.append((s, v))
        return out

    def _publish(self, tok, reads, writes):
        for r in reads:
            b = self.buf.setdefault(r, [None, []])
            lst = b[1]
            for i, t in enumerate(lst):
                if t[0] == tok[0]:
                    if t[1] < tok[1]:
                        lst[i] = tok
                    break
            else:
                lst.append(tok)
        for w in writes:
            self.buf[w] = [tok, []]

    def _split(self, reads, writes):
        def is_ps(k):
            n = k[0] if isinstance(k, tuple) else k
            return isinstance(n, str) and (n.startswith("ps") or n in ("pm", "pbq"))
        def expand(keys):
            out = []
            for k in keys:
                if isinstance(k, WKey):
                    out.extend(("w", s_) for s_ in k.slots)
                else:
                    out.append(k)
            return out
        for k in reads:
            if isinstance(k, WKey):
                for s_ in k.slots:
                    if self.wgen.get(s_) != k.gen:
                        raise RuntimeError("stale weight ring slot used: slot %d gen %d (current %r)" % (s_, k.gen, self.wgen.get(s_)))
        reads = expand(reads)
        writes = expand(writes)
        r2 = [k for k in reads if not is_ps(k)]
        w2 = list(writes) + [k for k in reads if is_ps(k)]
        return r2, w2

    def op(self, eng, fn, reads=(), writes=(), signal=True):
        reads, writes = self._split(reads, writes)
        waits = self._collect(eng, reads, writes)
        if signal:
            self.cnt[eng] += 1
            tok = (eng, self.cnt[eng])
            self.pending[eng] = False
        else:
            tok = (eng, self.cnt[eng] + 1)
            self.pending[eng] = True
        self.ops[eng].append((waits, fn, (eng, 1) if signal else None))
        self._publish(tok, reads, writes)
        return tok

    def dma(self, eng, fn, reads=(), writes=()):
        reads, writes = self._split(reads, writes)
        if eng == "pool":
            j = NHWSEM + self.dma_rr["sw"]
            self.dma_rr["sw"] = (self.dma_rr["sw"] + 1) % (NDMASEM - NHWSEM)
        else:
            j = self.dma_rr["hw"]
            self.dma_rr["hw"] = (self.dma_rr["hw"] + 1) % NHWSEM
        sname = "dma%d" % j
        waits = self._collect(eng, reads, writes)
        prev = self.dma_uses[j]
        if prev > 0 and self.known[eng].get(sname, 0) < 16 * prev:
            self.known[eng][sname] = 16 * prev
            waits.append((sname, 16 * prev))
        self.dma_uses[j] += 1
        tok = (sname, 16 * self.dma_uses[j])
        self.ops[eng].append((waits, fn, (sname, 16)))
        self._publish(tok, reads, writes)
        return tok

    def barrier(self):
        toks = []
        for e in ("pe", "act", "dve", "pool"):
            if self.pending[e]:
                raise RuntimeError("barrier with pending unsignalled op on " + e)
            if self.cnt[e] > 0:
                toks.append((e, self.cnt[e]))
        for j in range(NDMASEM):
            if self.dma_uses[j] > 0:
                toks.append(("dma%d" % j, 16 * self.dma_uses[j]))
        for e in ENGS:
            waits = []
            for s, v in toks:
                if e == "pe" and s == "pe":
                    continue
                if self.known[e].get(s, 0) < v:
                    self.known[e][s] = v
                    waits.append((s, v))
            if waits:
                self.ops[e].append((waits, None, None))

    def emit(self, sems, block):
        for e in ENGS:
            if self.pending[e]:
                raise RuntimeError("engine %s ends with unsignalled op" % e)

        def run(engname, engobj):
            for waits, fn, inc in self.ops[engname]:
                for s, v in waits:
                    engobj.wait_ge(sems[s], v)
                if fn is None:
                    continue
                ins = fn(engobj)
                if inc is not None:
                    ins.then_inc(sems[inc[0]], inc[1])

        block.tensor(lambda t: run("pe", t))
        block.scalar(lambda t: run("act", t))
        block.vector(lambda t: run("dve", t))
        block.gpsimd(lambda t: run("pool", t))
        block.sync(lambda t: run("sp", t))


def build_program(stop_after=None, dbg=None):
    nc = bass.Bass("TRN2", target_bir_lowering=False)
    P = Prog()

    def din(name, shape):
        return nc.dram_tensor(name, list(shape), F32, kind="ExternalInput").ap()

    x_d = din("xin", [S, D])
    ctx_d = din("ctxin", [L, D])
    cvec_d = din("cvec", [128, 32])
    wmod_d = din("wmod", [D, 6 * D])
    bmodT_d = din("bmodt", [128, 96])
    bmodrow_d = din("bmodrow", [1, 6 * D])
    win_d = din("win", [D, IN_COLS])
    sink_d = din("sinkbc", [128, NH])
    convw_d = din("convwt", [128, 16 * 31])
    pvec_d = din("pvec", [128, 16 * 3])
    wap_d = din("wap", [D, D])
    wcp_d = din("wcp", [D, D])
    wout_d = din("wout", [D, D])
    lnrow_d = din("lnrow", [4, D])
    wffi_d = din("wffi", [D, 2 * HID])
    wffo_d = din("wffo", [HID, D])
    cst_d = din("cst", [128, 128 * 3 + 1024])
    rope_d = din("rope", [2, 128, S])
    out_d = nc.dram_tensor("yout", [S, D], F32, kind="ExternalOutput").ap()
    gsc_d = nc.dram_tensor("gscr", [2, D], F32).ap()
    dbg_d = None
    if dbg is not None:
        dbg_d = nc.dram_tensor("dbgout", [128, dbg], F32, kind="ExternalOutput").ap()

    es = ExitStack()
    with es:
        sems = {n: es.enter_context(nc.semaphore(n)) for n in list(ENGS) + ["dma%d" % j for j in range(NDMASEM)]}

        def sbuf(name, shape, dt):
            return es.enter_context(nc.sbuf_tensor(name, list(shape), dt))

        A1N = 16 * EXT + 16 * HALF
        A1 = sbuf("a1", [128, A1N], BF16)
        A2 = sbuf("a2", [128, 16 * UW], BF16)
        A3 = sbuf("a3", [128, 16 * HALF], BF16)
        A4 = sbuf("a4", [128, 4 * EXT + NBLK * 512], BF16)
        KVC = sbuf("kvc", [128, 4 * L + 2 * 512], BF16)
        NSLOT = 4
        WR = sbuf("wr", [128, NSLOT * 4096], BF16)
        cstf = sbuf("cstf", [128, 128 * 3 + 1024], F32)
        cstb = sbuf("cstb", [128, 128 * 3 + 1024], BF16)
        modT = sbuf("modt", [128, 128], F32)
        bmodT = sbuf("bmodts", [128, 96], F32)
        sp1 = sbuf("sp1", [128, 64], F32)
        cvec = sbuf("cvecs", [128, 32], F32)
        svec = sbuf("svecs", [128, 32], F32)
        sinkb = sbuf("sinkb", [128, NH], F32)
        esink = sbuf("esink", [128, NH], F32)
        cwf = sbuf("cwf", [128, 16 * 31], F32)
        pvec = sbuf("pvecs", [128, 48], F32)
        stat = sbuf("stat", [128, 64], F32)
        xstat = sbuf("xstat", [128, 4 * NBLK * 2], F32)
        epsb = sbuf("epsb", [128, 1], F32)
        sa8 = sbuf("sa8", [128, 16], F32)
        lnsc = sbuf("lnsc", [128, 3 * 32], F32)
        lnrs = sbuf("lnrs", [128, 32], F32)
        ps = [es.enter_context(nc.psum_tensor("psb%d" % i, [128, 512], F32)) for i in range(8)]
        block = es.enter_context(nc.Block())

        ident = cstb[:, 0:128]
        ones = cstb[:, 128:256]
        perm = cstb[:, 256:384]
        maskp = cstb[:, 384:896]
        maskn = cstb[:, 896:1408]

        def a1v():
            hT = A1[:, 0:16 * EXT].rearrange("p (c t) -> p c t", c=16)
            QA = A1[:, 16 * EXT:A1N].rearrange("p (c t) -> p c t", c=16)
            return hT, QA

        hT, QA = a1v()
        PL = A1[:, 0:2 * 8 * D].bitcast(F32).rearrange("p (b d) -> p b d", b=8)
        UC = A2[:, :].rearrange("p (c t) -> p c t", c=16)
        MG = A3[:, :].rearrange("p (c t) -> p c t", c=16)
        kT = A4[:, 0:4 * EXT].rearrange("p (c t) -> p c t", c=4)
        vT = A4[:, 4 * EXT:4 * EXT + NBLK * 512].rearrange("p (b d) -> p b d", b=NBLK)
        kc = KVC[:, 0:4 * L].rearrange("p (c t) -> p c t", c=4)
        vc = KVC[:, 4 * L:4 * L + 1024].rearrange("p (b d) -> p b d", b=2)

        def f32view(arena, off_bf, n_f32):
            return arena[:, off_bf:off_bf + 2 * n_f32].bitcast(F32)

        def dma(q, out, in_, reads=(), writes=()):
            return P.dma(q, lambda e: e.dma_start(out=out, in_=in_), reads, writes)

        def mm(out, lhsT, rhs, start, stop, reads, writes, signal):
            P.op("pe", lambda e: e.matmul(out, lhsT=lhsT, rhs=rhs, start=start, stop=stop), reads, writes, signal)

        def tr(out, in_, reads, writes, signal):
            P.op("pe", lambda e: e.transpose(out=out, in_=in_, identity=ident), reads, writes, signal)

        def act(out, in_, func, reads, writes, scale=1.0, bias=0.0, accum=None):
            if accum is None:
                P.op("act", lambda e: e.activation(out=out, in_=in_, func=func, bias=bias, scale=scale), reads, writes)
            else:
                P.op("act", lambda e: e.activation(out=out, in_=in_, func=func, bias=bias, scale=scale, accum_out=accum), reads, writes)

        def tt(eng, out, in0, in1, op, reads, writes):
            P.op(eng, lambda e: e.tensor_tensor(out=out, in0=in0, in1=in1, op=op), reads, writes)

        def ts(eng, out, in0, s1, s2, op0, op1, reads, writes):
            if s2 is None:
                P.op(eng, lambda e: e.tensor_scalar(out=out, in0=in0, scalar1=s1, scalar2=None, op0=op0), reads, writes)
            else:
                P.op(eng, lambda e: e.tensor_scalar(out=out, in0=in0, scalar1=s1, scalar2=s2, op0=op0, op1=op1), reads, writes)

        def stt(eng, out, in0, scalar, in1, op0, op1, reads, writes):
            P.op(eng, lambda e: e.scalar_tensor_tensor(out=out, in0=in0, scalar=scalar, in1=in1, op0=op0, op1=op1), reads, writes)

        def cp(eng, out, in_, reads, writes):
            if eng == "act":
                P.op(eng, lambda e: e.activation(out=out, in_=in_, func=AF.Copy), reads, writes)
            else:
                P.op(eng, lambda e: e.tensor_copy(out=out, in_=in_), reads, writes)

        ring = {"pos": 0, "gen": 0}
        NHS = 2 * NSLOT

        def wload(dram_ap, shape3):
            n = shape3[0] * shape3[1]
            k = (n + 2047) // 2048
            p = ring["pos"] % NHS
            if p + k > NHS or (k == 2 and p % 2 == 1):
                ring["pos"] += (NHS - p) if p + k > NHS else 1
                p = ring["pos"] % NHS
            ring["pos"] += k
            ring["gen"] += 1
            view = WR[:, p * 2048:p * 2048 + n].rearrange("p (a b) -> p a b", a=shape3[0])
            key = WKey(("w", p))
            key.gen = ring["gen"]
            key.slots = tuple(range(p, p + k))
            for s_ in key.slots:
                P.wgen[s_] = key.gen
            dma("pool", view, dram_ap, writes=[key])
            return key, view

        def wcols(w_d, col0, ncols, krows=D):
            return w_d[:, col0:col0 + ncols].rearrange("(c p) n -> p c n", p=128), (krows // 128, ncols)

        P.op("pool", lambda e: e.memset(epsb[:, :], EPS), [], ["epsb"])
        dma("sp", cstf[:, :], cst_d, writes=["cstf"])
        cp("dve", cstb[:, :], cstf[:, :], ["cstf"], ["cst"])
        dma("sp", cvec[:, :], cvec_d, writes=["cvec"])
        dma("sp", bmodT[:, :], bmodT_d, writes=["bmodT"])
        dma("sp", sinkb[:, :], sink_d, writes=["sinkb"])
        dma("sp", cwf[:, :], convw_d, writes=["cwf"])
        dma("sp", pvec[:, :], pvec_d, writes=["pvec"])
        act(svec[:, :], cvec[:, :], AF.Silu, ["cvec"], ["svec"])
        act(esink[:, :], sinkb[:, :], AF.Exp, ["sinkb"], ["esink"])

        lnrot = {"n": 0}

        def ln_stats(xt_ap, xkey, rstd_ap, nmr_ap, skey):
            r = lnrot["n"] % 3
            lnrot["n"] += 1
            sc_ = lnsc[:, r * 32:(r + 1) * 32]
            st6 = sc_[:, 0:24].rearrange("p (a b) -> p a b", a=4)
            for a in range(4):
                P.op("dve", (lambda e, a=a: e.bn_stats(out=st6[:, a, :], in_=xt_ap[:, a * 512:(a + 1) * 512])), [xkey], [("st6", r)])
            P.op("dve", lambda e: e.bn_aggr(out=sc_[:, 24:26], in_=sc_[:, 0:24]), [("st6", r)], [("mv", r)])
            act(sc_[:, 26:27], sc_[:, 25:26], AF.Sqrt, [("mv", r)], [("sd", r)], bias=epsb[:, 0:1])
            P.op("dve", lambda e: e.reciprocal(out=rstd_ap, in_=sc_[:, 26:27]), [("sd", r)], [skey])
            stt("dve", nmr_ap, sc_[:, 24:25], -1.0, rstd_ap, ALU.mult, ALU.mult, [("mv", r), skey], [skey])

        xts_g = [f32view(A3, 0, D), f32view(A3, 2 * D, D)]
        xbs_g = [A3[:, 4 * D:5 * D], A3[:, 5 * D:6 * D]]

        def ln_block_g(src_ap, i, dstT, dst_col, SHf, SCf, xs_r, xs_n, tagk, raw=False):
            xt = xts_g[i % 2]
            xb = xbs_g[i % 2]
            dma("sp", xt, src_ap, writes=[("xt", i % 2)])
            ln_stats(xt, ("xt", i % 2), xs_r, xs_n, ("xs", tagk))
            act(xb, xt, AF.Identity, [("xt", i % 2), ("xs", tagk)], [("xb", i % 2)], scale=xs_r, bias=xs_n)
            for hb in range(2):
                bi_ = (2 * i + hb) % 4
                pbb = ps[bi_][:, :].bitcast(BF16)
                for j in range(8):
                    c = hb * 8 + j
                    tr(pbb[:, j * 128:(j + 1) * 128], xb[:, c * 128:(c + 1) * 128], [("xb", i % 2), "cst"], [("psA", bi_)], j == 7)
                if raw:
                    cp("act" if hb == 0 else "dve", dstT[:, hb * 8:hb * 8 + 8, dst_col:dst_col + 128],
                       pbb.rearrange("p (c t) -> p c t", c=8), [("psA", bi_)], [(tagk[0] + "T", c_) for c_ in range(hb * 8, hb * 8 + 8)])
                else:
                    for j in range(8):
                        c = hb * 8 + j
                        act(dstT[:, c, dst_col:dst_col + 128], pbb[:, j * 128:(j + 1) * 128], AF.Identity,
                            [("psA", bi_), "modT", "sp1"], [(tagk[0] + "T", c)], scale=SCf(c), bias=SHf(c))

        PREA = []

        def emit_pre_block(idx):
            kind, blk = PREA[idx]
            hT_, _ = a1v()
            if kind == "x":
                ln_block_g(x_d[blk * 128:(blk + 1) * 128, :], idx, hT_, blk * 128, None, None,
                           xstat[:, blk * 2:blk * 2 + 1], xstat[:, blk * 2 + 1:blk * 2 + 2], ("h", 0, blk), raw=True)
            else:
                hcT_ = A3[:, 6 * D:6 * D + 16 * L].rearrange("p (c t) -> p c t", c=16)
                ln_block_g(ctx_d[blk * 128:(blk + 1) * 128, :], idx, hcT_, blk * 128, None, None,
                           stat[:, 32 + 2 * blk:33 + 2 * blk], stat[:, 33 + 2 * blk:34 + 2 * blk], ("hc", 0, blk), raw=True)

        NB0 = 256
        wt0 = [WR[:, 0:8192].bitcast(F32).rearrange("p (c n) -> p c n", c=16),
               WR[:, 8192:16384].bitcast(F32).rearrange("p (c n) -> p c n", c=16)]
        QA0 = 16 * EXT
        NB_A2 = 9216

        def accB_cols(c0, n):
            if c0 + n <= NB_A2:
                return A2[:, 2 * c0:2 * (c0 + n)].bitcast(F32)
            assert c0 >= NB_A2
            o = QA0 + 2 * (c0 - NB_A2)
            return A1[:, o:o + 2 * n].bitcast(F32)

        def accC_cols(c0, n):
            o = QA0 + 2 * (6 * D - NB_A2) + 2 * c0
            return A1[:, o:o + 2 * n].bitcast(F32)

        nblk0 = 6 * D // NB0
        pre_blocks = [("x", b_) for b_ in range(NBLK)] + [("c", 0), ("c", 1)]
        pre_state = {"i": 0}
        for bi in range(nblk0):
            w = wt0[bi % 2]
            wk = ("w0", bi % 2)
            dma("sp", w, wmod_d[:, bi * NB0:(bi + 1) * NB0].rearrange("(c p) n -> p c n", p=128), writes=[wk])
            seg = accB_cols(bi * NB0, NB0)
            for c in range(16):
                if c == 0:
                    ts("dve", seg, w[:, 0, :], svec[:, 0:1], None, ALU.mult, None, [wk, "svec"], [("accB", bi)])
                else:
                    stt("dve", seg, w[:, c, :], svec[:, c:c + 1], seg, ALU.mult, ALU.add, [wk, "svec"], [("accB", bi)])
            if bi * NB0 < 2 * D:
                segc = accC_cols(bi * NB0, NB0)
                for c in range(16):
                    if c == 0:
                        act(segc, w[:, 0, :], AF.Copy, [wk, "svec"], [("accC", bi)], scale=svec[:, 16:17])
                    else:
                        pt_ = f32view(A4, 4096 + (c % 2) * 1024, NB0)
                        act(pt_, w[:, c, :], AF.Copy, [wk, "svec"], [("ptmp", c % 2)], scale=svec[:, 16 + c:17 + c])
                        tt("pool", segc, segc, pt_, ALU.add, [("ptmp", c % 2), ("accC", bi)], [("accC", bi)])
            if bi % 4 == 3 and pre_state["i"] < len(pre_blocks):
                PREA.append(pre_blocks[pre_state["i"]])
                pre_state["i"] += 1
                emit_pre_block(len(PREA) - 1)
        while pre_state["i"] < len(pre_blocks):
            PREA.append(pre_blocks[pre_state["i"]])
            pre_state["i"] += 1
            emit_pre_block(len(PREA) - 1)
        P.barrier()
        pm = ps[0]
        hi = WR[:, 0:1024]
        lo = WR[:, 1024:2048]
        groups = [("B", g_ * 1024, g_ * 8) for g_ in range(12)] + [("C", g_ * 1024, 96 + g_ * 8) for g_ in range(4)]
        for nm, c0, colb in groups:
            src = accB_cols(c0, 1024) if nm == "B" else accC_cols(c0, 1024)
            cp("act", hi, src, [], ["hi"])
            tt("dve", src, src, hi, ALU.subtract, ["hi"], ["resid"])
            cp("act", lo, src, ["resid"], ["lo"])
            for j in range(8):
                col = colb + j
                mm(pm[:, col:col + 1], hi[:, j * 128:(j + 1) * 128], ones[:, 0:1], True, False, ["hi", "cst"], ["pm"], False)
                mm(pm[:, col:col + 1], lo[:, j * 128:(j + 1) * 128], ones[:, 0:1], False, True, ["lo", "cst"], ["pm"], j == 7)
            if nm == "B" and (4096 <= c0 < 6144 or 10240 <= c0 < 12288):
                which = 0 if c0 < 6144 else 1
                vbase = 4096 if which == 0 else 10240
                for q2 in range(2):
                    q4 = (c0 - vbase) // 512 + q2
                    pb = ps[1 + (q4 % 2)]
                    mm(pb[:, :], ones, hi[:, q2 * 512:(q2 + 1) * 512], True, False, ["hi", "cst"], [("pbq", q4 % 2)], False)
                    mm(pb[:, :], ones, lo[:, q2 * 512:(q2 + 1) * 512], False, True, ["lo", "cst"], [("pbq", q4 % 2)], True)
                    gt_ = f32view(A4, (q4 % 2) * 1024, 512)
                    br_ = f32view(A4, 2048 + (q4 % 2) * 1024, 512)
                    gcol = (2 * D if which == 0 else 5 * D) + q4 * 512
                    dma("sp", br_[0:1, :], bmodrow_d[0:1, gcol:gcol + 512], writes=[("br", q4 % 2)])
                    tt("dve", gt_[0:1, :], pb[0:1, :], br_[0:1, :], ALU.add, [("pbq", q4 % 2), ("br", q4 % 2)], [("gt", q4 % 2)])
                    dma("sp", gsc_d[which:which + 1, q4 * 512:(q4 + 1) * 512], gt_[0:1, :], reads=[("gt", q4 % 2)], writes=[("gsc", which, q4)])
        tt("dve", modT[:, 0:96], pm[:, 0:96], bmodT[:, :], ALU.add, ["pm", "bmodT"], ["modT"])
        tt("dve", modT[:, 96:128], pm[:, 96:128], bmodT[:, 0:32], ALU.add, ["pm", "bmodT"], ["modT"])
        ts("dve", sp1[:, 0:16], modT[:, 16:32], 1.0, None, ALU.add, None, ["modT"], ["sp1"])
        ts("dve", sp1[:, 16:32], modT[:, 64:80], 1.0, None, ALU.add, None, ["modT"], ["sp1"])
        ts("dve", sp1[:, 32:48], modT[:, 112:128], 1.0, None, ALU.add, None, ["modT"], ["sp1"])
        ts("dve", sp1[:, 48:64], sp1[:, 16:32], 1.0 / ALPHA, None, ALU.mult, None, ["sp1"], ["sp1b"])
        P.barrier()
        hT0_, _ = a1v()
        hcT0_ = A3[:, 6 * D:6 * D + 16 * L].rearrange("p (c t) -> p c t", c=16)
        for c in range(16):
            if c % 2 == 0:
                act(hT0_[:, c, :], hT0_[:, c, :], AF.Identity, ["modT", "sp1"], [("hT", c)], scale=sp1[:, c:c + 1], bias=modT[:, c:c + 1])
                ts("dve", hcT0_[:, c, :], hcT0_[:, c, :], sp1[:, 32 + c:33 + c], modT[:, 96 + c:97 + c], ALU.mult, ALU.add, ["modT", "sp1"], [("hcT", c)])
            else:
                ts("dve", hT0_[:, c, :], hT0_[:, c, :], sp1[:, c:c + 1], modT[:, c:c + 1], ALU.mult, ALU.add, ["modT", "sp1"], [("hT", c)])
                act(hcT0_[:, c, :], hcT0_[:, c, :], AF.Identity, ["modT", "sp1"], [("hcT", c)], scale=sp1[:, 32 + c:33 + c], bias=modT[:, 96 + c:97 + c])
        P.barrier()
        SH1 = lambda c: modT[:, c:c + 1]
        SC1 = lambda c: sp1[:, c:c + 1]
        SH2 = lambda c: modT[:, 48 + c:49 + c]
        SC2 = lambda c: sp1[:, 48 + c:49 + c]
        CSH1 = lambda c: modT[:, 96 + c:97 + c]
        CSC1 = lambda c: sp1[:, 32 + c:33 + c]

        state = {"dbg_done": False, "off": 0}

        def ck(stage, items):
            if dbg is not None and items:
                dump(items)
            if stop_after == stage:
                state["dbg_done"] = True
                return True
            return False

        def dump(items):
            stg = f32view(WR, 0, 8192)
            P.barrier()
            for ap, n in items:
                off = state["off"]
                cp("dve", stg[:, 0:n], ap, [], ["stg"])
                dma("sp", dbg_d[:, off:off + n], stg[:, 0:n], reads=["stg"], writes=[("dbg", off)])
                state["off"] = off + n
            P.barrier()

        def emit_half(h):
            g0 = 0 if h == 0 else S - EXT
            m0 = 0 if h == 0 else 128
            gm0 = g0 + m0
            XS = lambda blk, k: xstat[:, h * 36 + blk * 2 + k:h * 36 + blk * 2 + k + 1]
            hT, QA = a1v()

            xts = [f32view(A3, 0, D), f32view(A3, 2 * D, D)]
            xbs = [A3[:, 4 * D:5 * D], A3[:, 5 * D:6 * D]]
            cos_t = f32view(A3, 5120, EXT)
            sin_t = f32view(A3, 5120 + 2 * EXT, EXT)
            tmpA = 6 * D

            hcT = A3[:, tmpA:tmpA + 16 * L].rearrange("p (c t) -> p c t", c=16) if h == 0 else None
            if h == 1:
                for blk in range(NBLK):
                    ln_block_g(x_d[g0 + blk * 128:g0 + (blk + 1) * 128, :], blk, hT, blk * 128, SH1, SC1,
                               XS(blk, 0), XS(blk, 1), ("h", h, blk))
            P.barrier()
            dma("sp", cos_t, rope_d[0, :, g0:g0 + EXT], writes=["cos"])
            dma("sp", sin_t, rope_d[1, :, g0:g0 + EXT], writes=["sin"])
            HK = lambda: [("hT", c) for c in range(16)]
            HCK = lambda: [("hcT", c) for c in range(16)]
            if h == 0 and ck("A", []):
                return

            tiles_ext = [(0, 512), (512, 512), (1024, 128)]
            tiles_main = [(m0, 512), (m0 + 512, 512)]
            tb32 = [f32view(A3, i * 1024, 512) for i in range(4)]
            tbb = [A3[:, 4096 + i * 512:4096 + (i + 1) * 512] for i in range(2)]
            bk = {"n": 0}

            def nb(keyname="B"):
                i = bk["n"] % 8
                bk["n"] += 1
                return ps[i], ("ps", i)

            def proj_fm(wview, wkey, j, t0, n, src=hT, srck=None):
                pb, pk = nb()
                for kc_ in range(16):
                    mm(pb[:, 0:n], wview[:, kc_, j * 128:(j + 1) * 128], src[:, kc_, t0:t0 + n], kc_ == 0, kc_ == 15,
                       [wkey, srck[kc_] if srck else ("hT", kc_)], [pk], kc_ == 15)
                return pb, pk

            def rope_evac(pb, pk, n, t0, dst_ap, dkey, ui):
                if os.environ.get("SKIP_ROPE"):
                    act(dst_ap, pb[:, 0:n], AF.Copy, [pk], [dkey])
                    return
                xb_ = tbb[ui % 2]
                act(xb_[:, 0:n], pb[:, 0:n], AF.Copy, [pk], [("tbb", ui % 2)])
                var = os.environ.get("ROPE_VAR", "")
                if var == "noperm":
                    pr, prk = pb, pk
                else:
                    pr, prk = nb()
                    mm(pr[:, 0:n], perm, xb_[:, 0:n], True, True, [("tbb", ui % 2), "cst"], [prk], True)
                if var == "nodve":
                    act(dst_ap, pr[:, 0:n], AF.Copy, [prk, pk], [dkey])
                    return
                t1 = tb32[(2 * ui) % 4]
                t2 = tb32[(2 * ui + 1) % 4]
                tt("dve", t1[:, 0:n], pb[:, 0:n], cos_t[:, t0:t0 + n], ALU.mult, [pk, "cos"], [("tb32", (2 * ui) % 4)])
                tt("dve", t2[:, 0:n], pr[:, 0:n], sin_t[:, t0:t0 + n], ALU.mult, [prk, "sin"], [("tb32", (2 * ui + 1) % 4)])
                tt("dve" if ui % 2 else "pool", dst_ap, t1[:, 0:n], t2[:, 0:n], ALU.add, [("tb32", (2 * ui) % 4), ("tb32", (2 * ui + 1) % 4)], [dkey])

            P.op("pool", lambda e: e.memset(UC[:, :, 0:15], 0.0), [], [("UCpad", 0)])
            P.op("pool", lambda e: e.memset(UC[:, :, 15 + EXT:UW], 0.0), [], [("UCpad", 1)])
            if h == 0 and ck("B0", []):
                return
            ui = 0
            loads = []
            for cp_ in range(8):
                loads.append((wcols(win_d, OFF_GLU + cp_ * 256, 256), wcols(win_d, OFF_GLU + D + cp_ * 256, 256)))

            def issue(i):
                (apa, sha), (apg, shg) = loads[i]
                return wload(apa, sha), wload(apg, shg)

            pend = issue(0)
            for cp_ in range(8):
                (ka, va), (kg, vg) = pend
                if cp_ + 1 < 8:
                    pend = issue(cp_ + 1)
                for j in range(2):
                    c = cp_ * 2 + j
                    for (t0, n) in tiles_ext:
                        pa, pak = proj_fm(va, ka, j, t0, n)
                        pg, pgk = proj_fm(vg, kg, j, t0, n)
                        sg = tb32[ui % 4]
                        act(sg[:, 0:n], pg[:, 0:n], AF.Sigmoid, [pgk], [("tb32", ui % 4)])
                        tt("dve", UC[:, c, 15 + t0:15 + t0 + n], pa[:, 0:n], sg[:, 0:n], ALU.mult, [pak, ("tb32", ui % 4)], [("UC", c)])
                        ui += 1
            if h == 0 and ck("B1a", []):
                return
            (kk0, vk0) = wload(*wcols(win_d, OFF_K, 256))
            (kk1, vk1) = wload(*wcols(win_d, OFF_K + 256, 256))
            (kv0, vv0) = wload(*wcols(win_d, OFF_V, 256))
            (kv1, vv1) = wload(*wcols(win_d, OFF_V + 256, 256))
            for g in range(4):
                wv_, wk_ = (vk0, kk0) if g < 2 else (vk1, kk1)
                for (t0, n) in tiles_ext:
                    pb, pk = proj_fm(wv_, wk_, g % 2, t0, n)
                    rope_evac(pb, pk, n, t0, kT[:, g, t0:t0 + n], ("kT", g), ui)
                    ui += 1
                if h == 0 and not os.environ.get("SKIP_CTXK"):
                    pb, pk = proj_fm(wv_, wk_, g % 2, 0, L, src=hcT, srck=HCK())
                    act(kc[:, g, :], pb[:, 0:L], AF.Copy, [pk], [("kc", g)])
            if h == 0 and ck("B1b", []):
                return
            nvb = NBLK + (2 if h == 0 else 0)
            for blk in range(nvb):
                pb, pk = nb()
                for hv, (wv_, wk_) in enumerate(((vv0, kv0), (vv1, kv1))):
                    for kc_ in range(16):
                        if blk < NBLK:
                            lhs = hT[:, kc_, blk * 128:(blk + 1) * 128]
                            rk = ("hT", kc_)
                        else:
                            lhs = hcT[:, kc_, (blk - NBLK) * 128:(blk - NBLK + 1) * 128]
                            rk = ("hcT", kc_)
                        mm(pb[:, hv * 256:(hv + 1) * 256], lhs, wv_[:, kc_, :], kc_ == 0, kc_ == 15, [wk_, rk], [pk], kc_ == 15)
                if blk < NBLK:
                    act(vT[:, blk, :], pb[:, :], AF.Copy, [pk], [("vT", blk)])
                else:
                    act(vc[:, blk - NBLK, :], pb[:, :], AF.Copy, [pk], [("vc", blk - NBLK)])
            if h == 0 and ck("B1", [(UC[:, 0, :], UW), (UC[:, 15, :], UW), (kT[:, 0, :], EXT), (kT[:, 3, :], EXT), (vT[:, 0, :], 512), (vT[:, 8, :], 512), (kc[:, 0, :], L), (vc[:, 1, :], 512)]):
                return
            pend = wload(*wcols(win_d, 0, 256))
            for hp in range(8):
                kq, vq = pend
                if hp + 1 < 8:
                    pend = wload(*wcols(win_d, (hp + 1) * 256, 256))
                for j in range(2):
                    hd = hp * 2 + j
                    for ti, (t0, n) in enumerate(tiles_main):
                        pb, pk = proj_fm(vq, kq, j, t0, n)
                        rope_evac(pb, pk, n, t0, QA[:, hd, ti * 512:ti * 512 + n], ("QA", hd, ti), ui)
                        ui += 1
            if h == 0 and ck("B", [(QA[:, 0, :], HALF), (QA[:, 15, :], HALF)]):
                return

            P.barrier()
            dgs = [A3[:, 0:31 * 128].rearrange("p (j m) -> p j m", j=31),
                   WR[:, 4096:4096 + 31 * 128].rearrange("p (j m) -> p j m", j=31)]
            yb32 = [f32view(A3, 4096 + i * 1024, 512) for i in range(2)]
            ysq = [A3[:, 4096 + 2048 + i * 512:4096 + 2048 + (i + 1) * 512] for i in range(2)]
            mean_bc = [f32view(A3, 8192 + i * 1024, 512) for i in range(2)]
            rstd_bc = [f32view(A3, 8192 + 2048 + i * 1024, 512) for i in range(2)]
            nmr_bc = [f32view(A3, 8192 + 4096 + i * 1024, 512) for i in range(2)]
            tC = [f32view(WR, i * 1024, 512) for i in range(4)]
            cw3 = cwf[:, :].rearrange("p (c j) -> p c j", c=16)
            u2 = 0
            for c in range(16):
                dg = dgs[c % 2]
                dgk = lambda j, c=c: ("dg", c % 2, j)
                for j in range(31):
                    if j % 2 == 0:
                        ts("dve", dg[:, j, :], ident, cw3[:, c, j:j + 1], None, ALU.mult, None, ["cst", "cwf"], [dgk(j)])
                    else:
                        act(dg[:, j, :], ident, AF.Copy, ["cst", "cwf"], [dgk(j)], scale=cw3[:, c, j:j + 1])
                pbs = []
                for ti, (t0, n) in enumerate(tiles_main):
                    pb, pk = ps[4 + (u2 % 4)], ("psC", u2 % 4)
                    u2 += 1
                    for j in range(31):
                        mm(pb[:, :], dg[:, j, :], UC[:, c, t0 + j:t0 + j + 512], j == 0, j == 30, [dgk(j), ("UC", c), ("UCpad", 0), ("UCpad", 1)], [pk], j == 30)
                    pbs.append((pb, pk, ti, t0))
                for (pb, pk, ti, t0) in pbs:
                    y32 = yb32[ti]
                    act(y32, pb[:, :], AF.Identity, [pk, "pvec"], [("y32", ti)], bias=pvec[:, c:c + 1])
                    cp("dve", UC[:, c, 15 + t0:15 + t0 + 512], y32, [("y32", ti)], [("UC", c)])
                    act(ysq[ti], y32, AF.Square, [("y32", ti)], [("ysq", ti)])
                    mm(ps[2 * ti][:, :], ones, UC[:, c, 15 + t0:15 + t0 + 512], c == 0, c == 15, ["cst", ("UC", c)], [("psS", 2 * ti)], True)
                    mm(ps[2 * ti + 1][:, :], ones, ysq[ti], c == 0, c == 15, ["cst", ("ysq", ti)], [("psS", 2 * ti + 1)], True)
            for ti, (t0, n) in enumerate(tiles_main):
                act(mean_bc[ti], ps[2 * ti][:, :], AF.Copy, [("psS", 2 * ti)], [("mean", ti)], scale=1.0 / D)
                tt("dve", tC[0], mean_bc[ti], mean_bc[ti], ALU.mult, [("mean", ti)], [("tC", 0)])
                stt("dve", tC[1], ps[2 * ti + 1][:, :], 1.0 / D, tC[0], ALU.mult, ALU.subtract, [("psS", 2 * ti + 1), ("tC", 0)], [("tC", 1)])
                act(tC[2], tC[1], AF.Sqrt, [("tC", 1)], [("tC", 2)], bias=epsb[:, 0:1])
                P.op("dve", (lambda e, ti=ti: e.reciprocal(out=rstd_bc[ti], in_=tC[2])), [("tC", 2)], [("rstd", ti)])
                stt("dve", nmr_bc[ti], mean_bc[ti], -1.0, rstd_bc[ti], ALU.mult, ALU.mult, [("mean", ti), ("rstd", ti)], [("nmr", ti)])
            u3 = 0
            for c in range(16):
                for ti, (t0, n) in enumerate(tiles_main):
                    ta = tC[u3 % 2]
                    tb_ = tC[2 + (u3 % 2)]
                    sl = UC[:, c, 15 + t0:15 + t0 + 512]
                    tt("dve", ta, sl, rstd_bc[ti], ALU.mult, [("UC", c), ("rstd", ti)], [("tC", u3 % 2)])
                    tt("pool", tb_, ta, nmr_bc[ti], ALU.add, [("tC", u3 % 2), ("nmr", ti)], [("tC", 2 + (u3 % 2))])
                    act(sl, tb_, AF.Silu, [("tC", 2 + (u3 % 2)), "pvec"], [("UC", c)], scale=pvec[:, 16 + c:17 + c], bias=pvec[:, 32 + c:33 + c])
                    u3 += 1
            if h == 0 and ck("C", [(UC[:, 0, :], UW), (UC[:, 15, :], UW)]):
                return

            P.barrier()
            pTb = [A3[:, i * 512:(i + 1) * 512] for i in range(6)]
            dsb = [f32view(A3, 3072 + i * 1024, 512) for i in range(2)]
            u4 = 0
            pj = 0
            for qb in range(8):
                lb = (m0 // 128) + qb
                gb = (gm0 // 128) + qb
                kbs = []
                if gb - 1 >= 0:
                    kbs.append(("w", lb - 1, maskp))
                kbs.append(("w", lb, None))
                if gb + 1 < 16:
                    kbs.append(("w", lb + 1, maskn))
                kbs.append(("c", 0, None))
                kbs.append(("c", 1, None))
                for g in range(4):
                    qap = QA[:, 4 * g:4 * g + 4, qb * 128:(qb + 1) * 128]
                    qkeys = [("QA", 4 * g + hh, qb // 4) for hh in range(4)]
                    ob, obk = ps[4 + 2 * (u4 % 2)], ("psO", 2 * (u4 % 2))
                    db, dbk = ps[5 + 2 * (u4 % 2)], ("psO", 2 * (u4 % 2) + 1)
                    pts = []
                    for (kind, kb, msk) in kbs:
                        sb_, sbk = ps[pj % 4], ("psQ", pj % 4)
                        if kind == "w":
                            lhs = kT[:, g, kb * 128:(kb + 1) * 128]
                            lk = ("kT", g)
                        else:
                            lhs = kc[:, g, kb * 128:(kb + 1) * 128]
                            lk = ("kc", g)
                        mm(sb_[:, :], lhs, qap, True, msk is None, [lk] + qkeys, [sbk], msk is None)
                        if msk is not None:
                            mm(sb_[:, :], ident, msk, False, True, ["cst"], [sbk], True)
                        pt = pTb[pj % 6]
                        act(pt, sb_[:, :], AF.Exp, [sbk], [("pT", pj % 6)], scale=SCALE)
                        pts.append((kind, kb, pt, ("pT", pj % 6)))
                        pj += 1
                    for i, (kind, kb, pt, ptk) in enumerate(pts):
                        if kind == "w":
                            lhs = vT[:, kb, g * 128:(g + 1) * 128]
                            lk = ("vT", kb)
                        else:
                            lhs = vc[:, kb, g * 128:(g + 1) * 128]
                            lk = ("vc", kb)
                        last = i == len(pts) - 1
                        mm(ob[:, :], lhs, pt, i == 0, last, [lk, ptk], [obk], last)
                        mm(db[:, :], ones, pt, i == 0, last, ["cst", ptk], [dbk], last)
                    ds_ = dsb[u4 % 2]
                    for hh in range(4):
                        ts("dve", ds_[:, hh * 128:(hh + 1) * 128], db[:, hh * 128:(hh + 1) * 128], esink[:, 4 * g + hh:4 * g + hh + 1], None,
                           ALU.add, None, [dbk, "esink"], [("ds", u4 % 2, hh)])
                    P.op("dve", (lambda e, ds_=ds_: e.reciprocal(out=ds_, in_=ds_)), [("ds", u4 % 2, hh) for hh in range(4)], [("dsr", u4 % 2)])
                    tt("dve", qap, ob[:, :].rearrange("p (h q) -> p h q", h=4), ds_.rearrange("p (h q) -> p h q", h=4), ALU.mult,
                       [obk, ("dsr", u4 % 2)], [("AO", 4 * g + hh, qb) for hh in range(4)] + qkeys)
                    u4 += 1
            AOK = lambda c: [("AO", c, qb) for qb in range(8)]
            if h == 0 and ck("D", [(QA[:, 0, :], HALF), (QA[:, 15, :], HALF)]):
                return

            P.barrier()
            e32 = [f32view(A4, i * 1024, 512) for i in range(8)]
            u5 = 0

            def issueE(f):
                return (wload(*wcols(wap_d, f * 128, 128)), wload(*wcols(wcp_d, f * 128, 128)),
                        wload(*wcols(win_d, OFF_GATE + f * 128, 128)), wload(*wcols(win_d, OFF_GATE + D + f * 128, 128)))

            pendE = issueE(0)
            for f in range(16):
                ws = pendE
                if f + 1 < 16:
                    pendE = issueE(f + 1)
                j = 0
                for ti, (t0, n) in enumerate(tiles_main):
                    (ka, va), (kcv, vcv), (kga, vga), (kgb, vgb) = ws
                    pa, pak = nb()
                    for kc_ in range(16):
                        mm(pa[:, :], va[:, kc_, 0:128], QA[:, kc_, ti * 512:(ti + 1) * 512], kc_ == 0, kc_ == 15,
                           [ka] + AOK(kc_), [pak], kc_ == 15)
                    pc, pck = nb()
                    for kc_ in range(16):
                        mm(pc[:, :], vcv[:, kc_, 0:128], UC[:, kc_, 15 + t0:15 + t0 + 512], kc_ == 0, kc_ == 15,
                           [kcv, ("UC", kc_)], [pck], kc_ == 15)
                    pga, pgak = proj_fm(vga, kga, 0, t0, 512)
                    pgb, pgbk = proj_fm(vgb, kgb, 0, t0, 512)
                    sA = e32[(4 * u5) % 8]
                    sB = e32[(4 * u5 + 1) % 8]
                    t1 = e32[(4 * u5 + 2) % 8]
                    t2 = e32[(4 * u5 + 3) % 8]
                    kA, kB, k1, k2 = [("e32", (4 * u5 + i) % 8) for i in range(4)]
                    act(sA, pga[:, :], AF.Sigmoid, [pgak], [kA])
                    act(sB, pgb[:, :], AF.Sigmoid, [pgbk], [kB])
                    tt("dve", t1, pa[:, :], sA, ALU.mult, [pak, kA], [k1])
                    tt("dve", t2, pc[:, :], sB, ALU.mult, [pck, kB], [k2])
                    tt("pool", MG[:, f, ti * 512:(ti + 1) * 512], t1, t2, ALU.add, [k1, k2], [("MG", f)])
                    u5 += 1
            if h == 0 and ck("E", [(MG[:, 0, :], HALF), (MG[:, 15, :], HALF)]):
                return

            P.barrier()
            xr = [f32view(A2, i * 2 * D, D) for i in range(2)]
            gbc = [f32view(A2, 4 * D + i * 1024, 512) for i in range(2)]
            f32t = [f32view(A2, 4 * D + 2048 + i * 1024, 512) for i in range(4)]
            lnb = [f32view(A2, i * 2 * D, D) for i in range(2)]
            for tb in range(8):
                i = tb % 2
                dma("sp", xr[i], x_d[gm0 + tb * 128:gm0 + (tb + 1) * 128, :], writes=[("xr", i)])
                lbk = m0 // 128 + tb
                ts("dve", sa8[:, 2 * tb:2 * tb + 1], XS(lbk, 0), ALPHA, None, ALU.mult, None, [], [("sa", tb, 0)])
                ts("dve", sa8[:, 2 * tb + 1:2 * tb + 2], XS(lbk, 1), ALPHA, None, ALU.mult, None, [], [("sa", tb, 1)])
                act(PL[:, tb, :], xr[i], AF.Identity, [("xr", i), ("sa", tb, 0), ("sa", tb, 1)], [("PL", tb)], scale=sa8[:, 2 * tb:2 * tb + 1], bias=sa8[:, 2 * tb + 1:2 * tb + 2])
            u6 = 0
            for cg in range(4):
                (k0, v0) = wload(*wcols(wout_d, cg * 512, 256))
                (k1_, v1_) = wload(*wcols(wout_d, cg * 512 + 256, 256))
                gb_ = gbc[cg % 2]
                dma("sp", gb_, gsc_d[0:1, cg * 512:(cg + 1) * 512].partition_broadcast(128), reads=[("gsc", 0, cg)], writes=[("gbc", cg % 2)])
                for tb in range(8):
                    pb, pk = nb()
                    for hv, (wv_, wk_) in enumerate(((v0, k0), (v1_, k1_))):
                        for kc_ in range(16):
                            mm(pb[:, hv * 256:(hv + 1) * 256], MG[:, kc_, tb * 128:(tb + 1) * 128], wv_[:, kc_, :], kc_ == 0, kc_ == 15,
                               [wk_, ("MG", kc_)], [pk], kc_ == 15)
                    t_ = f32t[u6 % 4]
                    tt("dve", t_, pb[:, :], gb_, ALU.mult, [pk, ("gbc", cg % 2)], [("f32t", u6 % 4)])
                    tt("dve", PL[:, tb, cg * 512:(cg + 1) * 512], PL[:, tb, cg * 512:(cg + 1) * 512], t_, ALU.add, [("f32t", u6 % 4), ("PL", tb)], [("PL", tb)])
                    u6 += 1
            P.barrier()
            h2T = A3[:, :].rearrange("p (c t) -> p c t", c=16)
            dma("sp", lnb[0], lnrow_d[0:1, :].partition_broadcast(128), writes=[("lnb", 0)])
            dma("sp", lnb[1], lnrow_d[1:2, :].partition_broadcast(128), writes=[("lnb", 1)])
            act(lnb[0], lnb[0], AF.Copy, [("lnb", 0)], [("lnb", 0)], scale=ALPHA)
            act(lnb[1], lnb[1], AF.Copy, [("lnb", 1)], [("lnb", 1)], scale=ALPHA)
            xb2 = [A2[:, 4 * D + i * D:4 * D + (i + 1) * D] for i in range(2)]
            for tb in range(8):
                i = tb % 2
                plt = PL[:, tb, :]
                ln_stats(plt, ("PL", tb), lnrs[:, 2 * tb:2 * tb + 1], lnrs[:, 2 * tb + 1:2 * tb + 2], ("s1", tb))
                act(plt, plt, AF.Identity, [("PL", tb), ("s1", tb)], [("PL", tb)], scale=lnrs[:, 2 * tb:2 * tb + 1], bias=lnrs[:, 2 * tb + 1:2 * tb + 2])
                tt("dve", plt, plt, lnb[0], ALU.mult, [("PL", tb), ("lnb", 0)], [("PL", tb)])
                tt("dve", plt, plt, lnb[1], ALU.add, [("PL", tb), ("lnb", 1)], [("PL", tb)])
                cp("act", xb2[i], plt, [("PL", tb)], [("xb2", i)])
                for hb in range(2):
                    pb = ps[(2 * tb + hb) % 4]
                    pbb = pb[:, :].bitcast(BF16)
                    pk = ("psA", (2 * tb + hb) % 4)
                    for j in range(8):
                        c = hb * 8 + j
                        tr(pbb[:, j * 128:(j + 1) * 128], xb2[i][:, c * 128:(c + 1) * 128], [("xb2", i), "cst"], [pk], j == 7)
                    for j in range(8):
                        c = hb * 8 + j
                        act(h2T[:, c, tb * 128:(tb + 1) * 128], pbb[:, j * 128:(j + 1) * 128], AF.Identity,
                            [pk, "modT", "sp1"], [("h2T", c)], scale=SC2(c), bias=SH2(c))
            if h == 0 and ck("F", [(PL[:, 0, :], D), (PL[:, 7, :], D), (h2T[:, 0, :], HALF), (h2T[:, 15, :], HALF)]):
                return

            P.barrier()
            g2f = f32view(A2, 0, D)
            dma("sp", g2f, gsc_d[1:2, :].partition_broadcast(128), reads=[("gsc", 1, q) for q in range(4)], writes=["g2f"])
            hid = [A2[:, 4096 + i * 4096:4096 + (i + 1) * 4096].rearrange("p (c t) -> p c t", c=4) for i in range(2)]
            gs32 = [f32view(A2, 12288 + i * 1024, 512) for i in range(2)]
            ft = [f32view(A2, 14336 + i * 1024, 512) for i in range(4)]
            NG = HID // 512
            u7 = 0
            u8 = 0

            def issueG(gi, sub):
                c0 = gi * 512 + sub * 256
                return (wload(*wcols(wffi_d, c0, 256)), wload(*wcols(wffi_d, HID + c0, 256)))

            def ffn_in(sub, wpair, hb_, gi):
                nonlocal u7
                (kg_, vg_), (ku_, vu_) = wpair
                for j in range(2):
                    jj = sub * 2 + j
                    for ti in range(2):
                        pg, pgk = nb()
                        for kc_ in range(16):
                            mm(pg[:, :], vg_[:, kc_, j * 128:(j + 1) * 128], h2T[:, kc_, ti * 512:(ti + 1) * 512], kc_ == 0, kc_ == 15, [kg_, ("h2T", kc_)], [pgk], kc_ == 15)
                        pu, puk = nb()
                        for kc_ in range(16):
                            mm(pu[:, :], vu_[:, kc_, j * 128:(j + 1) * 128], h2T[:, kc_, ti * 512:(ti + 1) * 512], kc_ == 0, kc_ == 15, [ku_, ("h2T", kc_)], [puk], kc_ == 15)
                        s_ = gs32[u7 % 2]
                        act(s_, pg[:, :], AF.Silu, [pgk], [("gs32", u7 % 2)])
                        tt("dve", hb_[:, jj, ti * 512:(ti + 1) * 512], pu[:, :], s_, ALU.mult, [puk, ("gs32", u7 % 2)], [("hid", gi % 2, jj)])
                        u7 += 1

            def wout_rows(gi, sub):
                r0 = gi * 512 + sub * 256
                return wload(wffo_d[r0:r0 + 256, :].rearrange("(c p) n -> p c n", p=128), (2, D))

            in_a = issueG(0, 0)
            for gi in range(NG):
                hb_ = hid[gi % 2]
                in_b = issueG(gi, 1)
                ffn_in(0, in_a, hb_, gi)
                outs_w = [wout_rows(gi, 0), wout_rows(gi, 1)]
                ffn_in(1, in_b, hb_, gi)
                if gi + 1 < NG:
                    in_a = issueG(gi + 1, 0)
                for tb in range(8):
                    for cg in range(4):
                        pb, pk = nb()
                        for jj in range(4):
                            ko, vo = outs_w[jj // 2]
                            mm(pb[:, :], hb_[:, jj, tb * 128:(tb + 1) * 128], vo[:, jj % 2, cg * 512:(cg + 1) * 512], jj == 0, jj == 3,
                               [ko, ("hid", gi % 2, jj)], [pk], jj == 3)
                        t_ = ft[u8 % 4]
                        tt("dve", t_, pb[:, :], g2f[:, cg * 512:(cg + 1) * 512], ALU.mult, [pk, "g2f"], [("ft", u8 % 4)])
                        tt("pool", PL[:, tb, cg * 512:(cg + 1) * 512], PL[:, tb, cg * 512:(cg + 1) * 512], t_, ALU.add, [("ft", u8 % 4), ("PL", tb)], [("PL", tb)])
                        u8 += 1
            P.barrier()
            l2 = [f32view(A2, 0, D), f32view(A2, 2 * D, D)]
            dma("sp", l2[0], lnrow_d[2:3, :].partition_broadcast(128), writes=[("l2", 0)])
            dma("sp", l2[1], lnrow_d[3:4, :].partition_broadcast(128), writes=[("l2", 1)])
            for tb in range(8):
                plt = PL[:, tb, :]
                ln_stats(plt, ("PL", tb), lnrs[:, 16 + 2 * tb:17 + 2 * tb], lnrs[:, 17 + 2 * tb:18 + 2 * tb], ("s2", tb))
                act(plt, plt, AF.Identity, [("PL", tb), ("s2", tb)], [("PL", tb)], scale=lnrs[:, 16 + 2 * tb:17 + 2 * tb], bias=lnrs[:, 17 + 2 * tb:18 + 2 * tb])
                tt("dve", plt, plt, l2[0], ALU.mult, [("PL", tb), ("l2", 0)], [("PL", tb)])
                tt("pool", plt, plt, l2[1], ALU.add, [("PL", tb), ("l2", 1)], [("PL", tb)])
                dma("sp", out_d[gm0 + tb * 128:gm0 + (tb + 1) * 128, :], plt, reads=[("PL", tb)], writes=[("out", h, tb)])
            P.barrier()

        for h in range(2):
            emit_half(h)
            if state["dbg_done"]:
                break
        if not state["dbg_done"]:
            P.barrier()
        P.emit(sems, block)
    return nc


def _consts():
    ident = np.eye(128, dtype=np.float32)
    ones = np.ones((128, 128), np.float32)
    perm = np.zeros((128, 128), np.float32)
    for m in range(128):
        half = (m // 32) % 2
        if half == 0:
            perm[m + 32, m] = -1.0
        else:
            perm[m - 32, m] = 1.0
    j = np.arange(128)[:, None]
    q = np.arange(128)[None, :]
    mp = np.where(j >= q, 0.0, -30000.0).astype(np.float32)
    mn = np.where(j <= q, 0.0, -30000.0).astype(np.float32)
    maskp = np.tile(mp, (1, 4))
    maskn = np.tile(mn, (1, 4))
    cst = np.concatenate([ident, ones, perm, maskp, maskn], axis=1).astype(np.float32)
    t = np.arange(S)
    row = (t // 64).astype(np.float32)
    col = (t % 64).astype(np.float32)
    inv = (10000.0 ** (-np.arange(32, dtype=np.float32) / 32.0)).astype(np.float32)
    ang = np.zeros((128, S), np.float32)
    for p in range(128):
        axis = p // 64
        f = p % 32
        pos = row if axis == 0 else col
        ang[p] = pos * inv[f]
    rope = np.stack([np.cos(ang), np.sin(ang)]).astype(np.float32)
    return cst, rope


def _pc(v):
    return np.ascontiguousarray(v.reshape(16, 128).T)


def make_in_maps(x, c, ctx, c_ctx, w_mod, b_mod, w_in, attn_sink, conv_w, conv_b, conv_norm_g, conv_norm_b,
                 w_attn_proj, w_conv_proj, w_out, ln1_g, ln1_b, w_ffn_in, w_ffn_out, ln2_g, ln2_b):
    f = lambda a: np.ascontiguousarray(np.asarray(a, dtype=np.float32))
    cst, rope = _consts()
    bm = f(b_mod[0])
    shared = {
        "wmod": f(w_mod[0]),
        "bmodt": np.ascontiguousarray(bm.reshape(96, 128).T),
        "bmodrow": bm.reshape(1, -1),
        "win": f(w_in[0]),
        "sinkbc": np.ascontiguousarray(np.broadcast_to(f(attn_sink[0])[None, :], (128, NH))),
        "convwt": np.ascontiguousarray(f(conv_w[0]).T.reshape(16, 128, 31).transpose(1, 0, 2).reshape(128, 16 * 31)),
        "pvec": np.concatenate([_pc(f(conv_b[0])), _pc(f(conv_norm_g[0])), _pc(f(conv_norm_b[0]))], axis=1),
        "wap": f(w_attn_proj[0]),
        "wcp": f(w_conv_proj[0]),
        "wout": f(w_out[0]),
        "lnrow": np.stack([f(ln1_g[0]), f(ln1_b[0]), f(ln2_g[0]), f(ln2_b[0])]),
        "wffi": f(w_ffn_in[0]),
        "wffo": f(w_ffn_out[0]),
        "cst": cst,
        "rope": rope,
    }
    maps = []
    xs = f(x)
    cs = f(c)
    cx = f(ctx)
    cc = _pc(f(c_ctx))
    for b in range(8):
        m = dict(shared)
        m["xin"] = xs[b]
        m["ctxin"] = cx[b]
        m["cvec"] = np.ascontiguousarray(np.concatenate([_pc(cs[b]), cc], axis=1))
        maps.append(m)
    return maps


_NC_CACHE = {}


def kernel(**inputs):
    if "nc" not in _NC_CACHE:
        _NC_CACHE["nc"] = build_program()
    nc = _NC_CACHE["nc"]
    maps = make_in_maps(**inputs)
    res = run_bass_kernel_spmd(nc, maps, core_ids=list(range(8)))
    return np.stack([np.asarray(r["yout"], dtype=np.float32) for r in res.results], axis=0)
```

```python
import os
import numpy as np
from contextlib import ExitStack
import concourse.bass as bass
import concourse.mybir as mybir
from concourse.bass_utils import run_bass_kernel_spmd

F32 = mybir.dt.float32
BF16 = mybir.dt.bfloat16
AF = mybir.ActivationFunctionType
ALU = mybir.AluOpType

D = 2048
S = 2048
L = 256
NH = 16
NKV = 4
HID = 5632
OFF_K, OFF_V, OFF_GLU, OFF_GATE, IN_COLS = 2048, 2560, 3072, 7168, 11264
ALPHA = 2.0 ** 0.25
EPS = 1e-6
HALF = 1024
EXT = 1152
NBLK = 9
UW = EXT + 30
SCALE = 128.0 ** -0.5

ENGS = ("pe", "act", "dve", "pool", "sp")
NDMASEM = 24
NHWSEM = 16


class WKey(tuple):
    gen = 0
    slots = ()


class Prog:
    def __init__(self):
        self.ops = {e: [] for e in ENGS}
        self.cnt = {e: 0 for e in ENGS}
        self.known = {e: {} for e in ENGS}
        self.buf = {}
        self.dma_uses = [0] * NDMASEM
        self.dma_rr = {"hw": 0, "sw": 0}
        self.pending = {e: False for e in ENGS}
        self.wgen = {}

    def _collect(self, eng, reads, writes):
        need = {}

        def add(tok):
            if tok is not None and need.get(tok[0], 0) < tok[1]:
                need[tok[0]] = tok[1]

        for r in reads:
            b = self.buf.get(r)
            if b:
                add(b[0])
        for w in writes:
            b = self.buf.get(w)
            if b:
                add(b[0])
                for t in b[1]:
                    add(t)
        out = []
        kn = self.known[eng]
        for s, v in need.items():
            if eng == "pe" and s == "pe":
                continue
            if kn.get(s, 0) >= v:
                continue
            kn[s] = v
            out.append((s, v))
        return out

    def _publish(self, tok, reads, writes):
        for r in reads:
            b = self.buf.setdefault(r, [None, []])
            lst = b[1]
            for i, t in enumerate(lst):
                if t[0] == tok[0]:
                    if t[1] < tok[1]:
                        lst[i] = tok
                    break
            else:
                lst.append(tok)
        for w in writes:
            self.buf[w] = [tok, []]

    def _split(self, reads, writes):
        def is_ps(k):
            n = k[0] if isinstance(k, tuple) else k
            return isinstance(n, str) and (n.startswith("ps") or n in ("pm", "pbq"))
        def expand(keys):
            out = []
            for k in keys:
                if isinstance(k, WKey):
                    out.extend(("w", s_) for s_ in k.slots)
                else:
                    out.append(k)
            return out
        for k in reads:
            if isinstance(k, WKey):
                for s_ in k.slots:
                    if self.wgen.get(s_) != k.gen:
                        raise RuntimeError("stale weight ring slot used: slot %d gen %d (current %r)" % (s_, k.gen, self.wgen.get(s_)))
        reads = expand(reads)
        writes = expand(writes)
        r2 = [k for k in reads if not is_ps(k)]
        w2 = list(writes) + [k for k in reads if is_ps(k)]
        return r2, w2

    def op(self, eng, fn, reads=(), writes=(), signal=True):
        reads, writes = self._split(reads, writes)
        waits = self._collect(eng, reads, writes)
        if signal:
            self.cnt[eng] += 1
            tok = (eng, self.cnt[eng])
            self.pending[eng] = False
        else:
            tok = (eng, self.cnt[eng] + 1)
            self.pending[eng] = True
        self.ops[eng].append((waits, fn, (eng, 1) if signal else None))
        self._publish(tok, reads, writes)
        return tok

    def dma(self, eng, fn, reads=(), writes=()):
        reads, writes = self._split(reads, writes)
        if eng == "pool":
            j = NHWSEM + self.dma_rr["sw"]
            self.dma_rr["sw"] = (self.dma_rr["sw"] + 1) % (NDMASEM - NHWSEM)
        else:
            j = self.dma_rr["hw"]
            self.dma_rr["hw"] = (self.dma_rr["hw"] + 1) % NHWSEM
        sname = "dma%d" % j
        waits = self._collect(eng, reads, writes)
        prev = self.dma_uses[j]
        if prev > 0 and self.known[eng].get(sname, 0) < 16 * prev:
            self.known[eng][sname] = 16 * prev
            waits.append((sname, 16 * prev))
        self.dma_uses[j] += 1
        tok = (sname, 16 * self.dma_uses[j])
        self.ops[eng].append((waits, fn, (sname, 16)))
        self._publish(tok, reads, writes)
        return tok

    def barrier(self):
        toks = []
        for e in ("pe", "act", "dve", "pool"):
            if self.pending[e]:
                raise RuntimeError("barrier with pending unsignalled op on " + e)
            if self.cnt[e] > 0:
                toks.append((e, self.cnt[e]))
        for j in range(NDMASEM):
            if self.dma_uses[j] > 0:
                toks.append(("dma%d" % j, 16 * self.dma_uses[j]))
        for e in ENGS:
            waits = []
            for s, v in toks:
                if e == "pe" and s == "pe":
                    continue
                if self.known[e].get(s, 0) < v:
                    self.known[e][s] = v
                    waits.append((s, v))
            if waits:
                self.ops[e].append((waits, None, None))

    def emit(self, sems, block):
        for e in ENGS:
            if self.pending[e]:
                raise RuntimeError("engine %s ends with unsignalled op" % e)

        def run(engname, engobj):
            for waits, fn, inc in self.ops[engname]:
                for s, v in waits:
                    engobj.wait_ge(sems[s], v)
                if fn is None:
                    continue
                ins = fn(engobj)
                if inc is not None:
                    ins.then_inc(sems[inc[0]], inc[1])

        block.tensor(lambda t: run("pe", t))
        block.scalar(lambda t: run("act", t))
        block.vector(lambda t: run("dve", t))
        block.gpsimd(lambda t: run("pool", t))
        block.sync(lambda t: run("sp", t))


def build_program(stop_after=None, dbg=None):
    nc = bass.Bass("TRN2", target_bir_lowering=False)
    P = Prog()

    def din(name, shape):
        return nc.dram_tensor(name, list(shape), F32, kind="ExternalInput").ap()

    x_d = din("xin", [S, D])
    ctx_d = din("ctxin", [L, D])
    cvec_d = din("cvec", [128, 32])
    wmod_d = din("wmod", [D, 6 * D])
    bmodT_d = din("bmodt", [128, 96])
    bmodrow_d = din("bmodrow", [1, 6 * D])
    win_d = din("win", [D, IN_COLS])
    sink_d = din("sinkbc", [128, NH])
    convw_d = din("convwt", [128, 16 * 31])
    pvec_d = din("pvec", [128, 16 * 3])
    wap_d = din("wap", [D, D])
    wcp_d = din("wcp", [D, D])
    wout_d = din("wout", [D, D])
    lnrow_d = din("lnrow", [4, D])
    wffi_d = din("wffi", [D, 2 * HID])
    wffo_d = din("wffo", [HID, D])
    cst_d = din("cst", [128, 128 * 3 + 1024])
    rope_d = din("rope", [2, 128, S])
    out_d = nc.dram_tensor("yout", [S, D], F32, kind="ExternalOutput").ap()
    gsc_d = nc.dram_tensor("gscr", [2, D], F32).ap()
    dbg_d = None
    if dbg is not None:
        dbg_d = nc.dram_tensor("dbgout", [128, dbg], F32, kind="ExternalOutput").ap()

    es = ExitStack()
    with es:
        sems = {n: es.enter_context(nc.semaphore(n)) for n in list(ENGS) + ["dma%d" % j for j in range(NDMASEM)]}

        def sbuf(name, shape, dt):
            return es.enter_context(nc.sbuf_tensor(name, list(shape), dt))

        A1N = 16 * EXT + 16 * HALF
        A1 = sbuf("a1", [128, A1N], BF16)
        A2 = sbuf("a2", [128, 16 * UW], BF16)
        A3 = sbuf("a3", [128, 16 * HALF], BF16)
        A4 = sbuf("a4", [128, 4 * EXT + NBLK * 512], BF16)
        KVC = sbuf("kvc", [128, 4 * L + 2 * 512], BF16)
        NSLOT = 4
        WR = sbuf("wr", [128, NSLOT * 4096], BF16)
        cstf = sbuf("cstf", [128, 128 * 3 + 1024], F32)
        cstb = sbuf("cstb", [128, 128 * 3 + 1024], BF16)
        modT = sbuf("modt", [128, 128], F32)
        bmodT = sbuf("bmodts", [128, 96], F32)
        sp1 = sbuf("sp1", [128, 64], F32)
        cvec = sbuf("cvecs", [128, 32], F32)
        svec = sbuf("svecs", [128, 32], F32)
        sinkb = sbuf("sinkb", [128, NH], F32)
        esink = sbuf("esink", [128, NH], F32)
        cwf = sbuf("cwf", [128, 16 * 31], F32)
        pvec = sbuf("pvecs", [128, 48], F32)
        stat = sbuf("stat", [128, 64], F32)
        xstat = sbuf("xstat", [128, 4 * NBLK * 2], F32)
        epsb = sbuf("epsb", [128, 1], F32)
        sa8 = sbuf("sa8", [128, 16], F32)
        ps = [es.enter_context(nc.psum_tensor("psb%d" % i, [128, 512], F32)) for i in range(8)]
        block = es.enter_context(nc.Block())

        ident = cstb[:, 0:128]
        ones = cstb[:, 128:256]
        perm = cstb[:, 256:384]
        maskp = cstb[:, 384:896]
        maskn = cstb[:, 896:1408]

        def a1v():
            hT = A1[:, 0:16 * EXT].rearrange("p (c t) -> p c t", c=16)
            QA = A1[:, 16 * EXT:A1N].rearrange("p (c t) -> p c t", c=16)
            return hT, QA

        hT, QA = a1v()
        PL = A1[:, 0:2 * 8 * D].bitcast(F32).rearrange("p (b d) -> p b d", b=8)
        UC = A2[:, :].rearrange("p (c t) -> p c t", c=16)
        MG = A3[:, :].rearrange("p (c t) -> p c t", c=16)
        kT = A4[:, 0:4 * EXT].rearrange("p (c t) -> p c t", c=4)
        vT = A4[:, 4 * EXT:4 * EXT + NBLK * 512].rearrange("p (b d) -> p b d", b=NBLK)
        kc = KVC[:, 0:4 * L].rearrange("p (c t) -> p c t", c=4)
        vc = KVC[:, 4 * L:4 * L + 1024].rearrange("p (b d) -> p b d", b=2)

        def f32view(arena, off_bf, n_f32):
            return arena[:, off_bf:off_bf + 2 * n_f32].bitcast(F32)

        def dma(q, out, in_, reads=(), writes=()):
            return P.dma(q, lambda e: e.dma_start(out=out, in_=in_), reads, writes)

        def mm(out, lhsT, rhs, start, stop, reads, writes, signal):
            P.op("pe", lambda e: e.matmul(out, lhsT=lhsT, rhs=rhs, start=start, stop=stop), reads, writes, signal)

        def tr(out, in_, reads, writes, signal):
            P.op("pe", lambda e: e.transpose(out=out, in_=in_, identity=ident), reads, writes, signal)

        def act(out, in_, func, reads, writes, scale=1.0, bias=0.0, accum=None):
            if accum is None:
                P.op("act", lambda e: e.activation(out=out, in_=in_, func=func, bias=bias, scale=scale), reads, writes)
            else:
                P.op("act", lambda e: e.activation(out=out, in_=in_, func=func, bias=bias, scale=scale, accum_out=accum), reads, writes)

        def tt(eng, out, in0, in1, op, reads, writes):
            P.op(eng, lambda e: e.tensor_tensor(out=out, in0=in0, in1=in1, op=op), reads, writes)

        def ts(eng, out, in0, s1, s2, op0, op1, reads, writes):
            if s2 is None:
                P.op(eng, lambda e: e.tensor_scalar(out=out, in0=in0, scalar1=s1, scalar2=None, op0=op0), reads, writes)
            else:
                P.op(eng, lambda e: e.tensor_scalar(out=out, in0=in0, scalar1=s1, scalar2=s2, op0=op0, op1=op1), reads, writes)

        def stt(eng, out, in0, scalar, in1, op0, op1, reads, writes):
            P.op(eng, lambda e: e.scalar_tensor_tensor(out=out, in0=in0, scalar=scalar, in1=in1, op0=op0, op1=op1), reads, writes)

        def cp(eng, out, in_, reads, writes):
            if eng == "act":
                P.op(eng, lambda e: e.activation(out=out, in_=in_, func=AF.Copy), reads, writes)
            else:
                P.op(eng, lambda e: e.tensor_copy(out=out, in_=in_), reads, writes)

        ring = {"pos": 0, "gen": 0}
        NHS = 2 * NSLOT

        def wload(dram_ap, shape3):
            n = shape3[0] * shape3[1]
            k = (n + 2047) // 2048
            p = ring["pos"] % NHS
            if p + k > NHS or (k == 2 and p % 2 == 1):
                ring["pos"] += (NHS - p) if p + k > NHS else 1
                p = ring["pos"] % NHS
            ring["pos"] += k
            ring["gen"] += 1
            view = WR[:, p * 2048:p * 2048 + n].rearrange("p (a b) -> p a b", a=shape3[0])
            key = WKey(("w", p))
            key.gen = ring["gen"]
            key.slots = tuple(range(p, p + k))
            for s_ in key.slots:
                P.wgen[s_] = key.gen
            dma("pool", view, dram_ap, writes=[key])
            return key, view

        def wcols(w_d, col0, ncols, krows=D):
            return w_d[:, col0:col0 + ncols].rearrange("(c p) n -> p c n", p=128), (krows // 128, ncols)

        P.op("pool", lambda e: e.memset(epsb[:, :], EPS), [], ["epsb"])
        dma("sp", cstf[:, :], cst_d, writes=["cstf"])
        cp("dve", cstb[:, :], cstf[:, :], ["cstf"], ["cst"])
        dma("sp", cvec[:, :], cvec_d, writes=["cvec"])
        dma("sp", bmodT[:, :], bmodT_d, writes=["bmodT"])
        dma("sp", sinkb[:, :], sink_d, writes=["sinkb"])
        dma("sp", cwf[:, :], convw_d, writes=["cwf"])
        dma("sp", pvec[:, :], pvec_d, writes=["pvec"])
        act(svec[:, :], cvec[:, :], AF.Silu, ["cvec"], ["svec"])
        act(esink[:, :], sinkb[:, :], AF.Exp, ["sinkb"], ["esink"])

        NB0 = 512
        wt0 = [A3[:, 0:16384].bitcast(F32).rearrange("p (c n) -> p c n", c=16),
               A2[:, 0:16384].bitcast(F32).rearrange("p (c n) -> p c n", c=16)]
        accB = A1[:, 0:2 * 6 * D].bitcast(F32)
        accC = A1[:, 2 * 6 * D:2 * 6 * D + 2 * 2 * D].bitcast(F32)
        hi = WR[:, 0:4096]
        lo = WR[:, 4096:8192]
        nblk0 = 6 * D // NB0
        for bi in range(nblk0):
            w = wt0[bi % 2]
            wk = ("w0", bi % 2)
            dma("sp", w, wmod_d[:, bi * NB0:(bi + 1) * NB0].rearrange("(c p) n -> p c n", p=128), writes=[wk])
            eng = "dve"
            seg = accB[:, bi * NB0:(bi + 1) * NB0]
            for c in range(16):
                if c == 0:
                    ts(eng, seg, w[:, 0, :], svec[:, 0:1], None, ALU.mult, None, [wk, "svec"], [("accB", bi)])
                else:
                    stt(eng, seg, w[:, c, :], svec[:, c:c + 1], seg, ALU.mult, ALU.add, [wk, "svec"], [("accB", bi)])
            if bi * NB0 < 2 * D:
                segc = accC[:, bi * NB0:(bi + 1) * NB0]
                for c in range(16):
                    if c == 0:
                        ts("dve", segc, w[:, 0, :], svec[:, 16:17], None, ALU.mult, None, [wk, "svec"], [("accC", bi)])
                    else:
                        stt("dve", segc, w[:, c, :], svec[:, 16 + c:17 + c], segc, ALU.mult, ALU.add, [wk, "svec"], [("accC", bi)])
        P.barrier()
        pm = ps[0]
        for src, ncols, col_off, nm in ((accB, 6 * D, 0, "B"), (accC, 2 * D, 96, "C")):
            for g0 in range(0, ncols, 4096):
                n = min(4096, ncols - g0)
                cp("act", hi[:, 0:n], src[:, g0:g0 + n], [], ["hi"])
                tt("dve", src[:, g0:g0 + n], src[:, g0:g0 + n], hi[:, 0:n], ALU.subtract, ["hi"], ["resid"])
                cp("act", lo[:, 0:n], src[:, g0:g0 + n], ["resid"], ["lo"])
                for j in range(n // 128):
                    col = col_off + (g0 // 128) + j
                    mm(pm[:, col:col + 1], hi[:, j * 128:(j + 1) * 128], ones[:, 0:1], True, False, ["hi", "cst"], ["pm"], False)
                    mm(pm[:, col:col + 1], lo[:, j * 128:(j + 1) * 128], ones[:, 0:1], False, True, ["lo", "cst"], ["pm"], j == n // 128 - 1)
                if nm == "B" and g0 in (4096, 8192):
                    which = 0 if g0 == 4096 else 1
                    base = 0 if g0 == 4096 else 2048
                    for q4 in range(4):
                        pb = ps[1 + (q4 % 2)]
                        c0 = base + q4 * 512
                        mm(pb[:, :], ones, hi[:, c0:c0 + 512], True, False, ["hi", "cst"], [("pbq", q4 % 2)], False)
                        mm(pb[:, :], ones, lo[:, c0:c0 + 512], False, True, ["lo", "cst"], [("pbq", q4 % 2)], True)
                        gt_ = f32view(A4, (q4 % 2) * 1024, 512)
                        br_ = f32view(A4, 2048 + (q4 % 2) * 1024, 512)
                        gcol = (2 * D if which == 0 else 5 * D) + q4 * 512
                        dma("sp", br_[0:1, :], bmodrow_d[0:1, gcol:gcol + 512], writes=[("br", q4 % 2)])
                        tt("dve", gt_[0:1, :], pb[0:1, :], br_[0:1, :], ALU.add, [("pbq", q4 % 2), ("br", q4 % 2)], [("gt", q4 % 2)])
                        dma("sp", gsc_d[which:which + 1, q4 * 512:(q4 + 1) * 512], gt_[0:1, :], reads=[("gt", q4 % 2)], writes=[("gsc", which, q4)])
        tt("dve", modT[:, 0:96], pm[:, 0:96], bmodT[:, :], ALU.add, ["pm", "bmodT"], ["modT"])
        tt("dve", modT[:, 96:128], pm[:, 96:128], bmodT[:, 0:32], ALU.add, ["pm", "bmodT"], ["modT"])
        ts("dve", sp1[:, 0:16], modT[:, 16:32], 1.0, None, ALU.add, None, ["modT"], ["sp1"])
        ts("dve", sp1[:, 16:32], modT[:, 64:80], 1.0, None, ALU.add, None, ["modT"], ["sp1"])
        ts("dve", sp1[:, 32:48], modT[:, 112:128], 1.0, None, ALU.add, None, ["modT"], ["sp1"])
        ts("dve", sp1[:, 48:64], sp1[:, 16:32], 1.0 / ALPHA, None, ALU.mult, None, ["sp1"], ["sp1b"])
        P.barrier()
        SH1 = lambda c: modT[:, c:c + 1]
        SC1 = lambda c: sp1[:, c:c + 1]
        SH2 = lambda c: modT[:, 48 + c:49 + c]
        SC2 = lambda c: sp1[:, 48 + c:49 + c]
        CSH1 = lambda c: modT[:, 96 + c:97 + c]
        CSC1 = lambda c: sp1[:, 32 + c:33 + c]

        state = {"dbg_done": False, "off": 0}

        def ck(stage, items):
            if dbg is not None and items:
                dump(items)
            if stop_after == stage:
                state["dbg_done"] = True
                return True
            return False

        def dump(items):
            stg = f32view(WR, 0, 8192)
            P.barrier()
            for ap, n in items:
                off = state["off"]
                cp("dve", stg[:, 0:n], ap, [], ["stg"])
                dma("sp", dbg_d[:, off:off + n], stg[:, 0:n], reads=["stg"], writes=[("dbg", off)])
                state["off"] = off + n
            P.barrier()

        def ln_stats(xt_ap, xkey, rstd_ap, nmr_ap, skey):
            st6 = stat[:, 0:24].rearrange("p (a b) -> p a b", a=4)
            for a in range(4):
                P.op("dve", (lambda e, a=a: e.bn_stats(out=st6[:, a, :], in_=xt_ap[:, a * 512:(a + 1) * 512])), [xkey], ["st6"])
            P.op("dve", lambda e: e.bn_aggr(out=stat[:, 24:26], in_=stat[:, 0:24]), ["st6"], ["mv"])
            act(stat[:, 26:27], stat[:, 25:26], AF.Sqrt, ["mv"], ["sd"], bias=epsb[:, 0:1])
            P.op("dve", lambda e: e.reciprocal(out=rstd_ap, in_=stat[:, 26:27]), ["sd"], [skey])
            stt("dve", nmr_ap, stat[:, 24:25], -1.0, rstd_ap, ALU.mult, ALU.mult, ["mv", skey], [skey])

        def emit_half(h):
            g0 = 0 if h == 0 else S - EXT
            m0 = 0 if h == 0 else 128
            gm0 = g0 + m0
            XS = lambda blk, k: xstat[:, h * 36 + blk * 2 + k:h * 36 + blk * 2 + k + 1]
            hT, QA = a1v()

            xts = [f32view(A3, 0, D), f32view(A3, 2 * D, D)]
            xbs = [A3[:, 4 * D:5 * D], A3[:, 5 * D:6 * D]]
            cos_t = f32view(A3, 5120, EXT)
            sin_t = f32view(A3, 5120 + 2 * EXT, EXT)
            tmpA = 6 * D

            def ln_block(src_ap, blk, i, dstT, dst_col, SHf, SCf, xs_r, xs_n, tagk):
                xt = xts[i % 2]
                xb = xbs[i % 2]
                dma("sp", xt, src_ap, writes=[("xt", i % 2)])
                ln_stats(xt, ("xt", i % 2), xs_r, xs_n, ("xs", tagk))
                act(xb, xt, AF.Identity, [("xt", i % 2), ("xs", tagk)], [("xb", i % 2)], scale=xs_r, bias=xs_n)
                for hb in range(2):
                    pb = ps[(2 * i + hb) % 4]
                    pbb = pb[:, :].bitcast(BF16)
                    for j in range(8):
                        c = hb * 8 + j
                        tr(pbb[:, j * 128:(j + 1) * 128], xb[:, c * 128:(c + 1) * 128], [("xb", i % 2), "cst"], [("psA", (2 * i + hb) % 4)], j == 7)
                    for j in range(8):
                        c = hb * 8 + j
                        act(dstT[:, c, dst_col:dst_col + 128], pbb[:, j * 128:(j + 1) * 128], AF.Identity,
                            [("psA", (2 * i + hb) % 4), "modT", "sp1"], [(tagk[0] + "T", c)], scale=SCf(c), bias=SHf(c))

            for blk in range(NBLK):
                ln_block(x_d[g0 + blk * 128:g0 + (blk + 1) * 128, :], blk, blk, hT, blk * 128, SH1, SC1,
                         XS(blk, 0), XS(blk, 1), ("h", h, blk))
            hcT = A3[:, tmpA:tmpA + 16 * L].rearrange("p (c t) -> p c t", c=16) if h == 0 else None
            if h == 0:
                for blk in range(2):
                    ln_block(ctx_d[blk * 128:(blk + 1) * 128, :], blk, NBLK + blk, hcT, blk * 128, CSH1, CSC1,
                             stat[:, 32 + 2 * blk:33 + 2 * blk], stat[:, 33 + 2 * blk:34 + 2 * blk], ("hc", h, blk))
            P.barrier()
            dma("sp", cos_t, rope_d[0, :, g0:g0 + EXT], writes=["cos"])
            dma("sp", sin_t, rope_d[1, :, g0:g0 + EXT], writes=["sin"])
            HK = lambda: [("hT", c) for c in range(16)]
            HCK = lambda: [("hcT", c) for c in range(16)]
            if h == 0 and ck("A", []):
                return

            tiles_ext = [(0, 512), (512, 512), (1024, 128)]
            tiles_main = [(m0, 512), (m0 + 512, 512)]
            tb32 = [f32view(A3, i * 1024, 512) for i in range(4)]
            tbb = [A3[:, 4096 + i * 512:4096 + (i + 1) * 512] for i in range(2)]
            bk = {"n": 0}

            def nb(keyname="B"):
                i = bk["n"] % 8
                bk["n"] += 1
                return ps[i], ("ps", i)

            def proj_fm(wview, wkey, j, t0, n, src=hT, srck=None):
                pb, pk = nb()
                for kc_ in range(16):
                    mm(pb[:, 0:n], wview[:, kc_, j * 128:(j + 1) * 128], src[:, kc_, t0:t0 + n], kc_ == 0, kc_ == 15,
                       [wkey, srck[kc_] if srck else ("hT", kc_)], [pk], kc_ == 15)
                return pb, pk

            def rope_evac(pb, pk, n, t0, dst_ap, dkey, ui):
                if os.environ.get("SKIP_ROPE"):
                    act(dst_ap, pb[:, 0:n], AF.Copy, [pk], [dkey])
                    return
                xb_ = tbb[ui % 2]
                act(xb_[:, 0:n], pb[:, 0:n], AF.Copy, [pk], [("tbb", ui % 2)])
                var = os.environ.get("ROPE_VAR", "")
                if var == "noperm":
                    pr, prk = pb, pk
                else:
                    pr, prk = nb()
                    mm(pr[:, 0:n], perm, xb_[:, 0:n], True, True, [("tbb", ui % 2), "cst"], [prk], True)
                if var == "nodve":
                    act(dst_ap, pr[:, 0:n], AF.Copy, [prk, pk], [dkey])
                    return
                t1 = tb32[(2 * ui) % 4]
                t2 = tb32[(2 * ui + 1) % 4]
                tt("dve", t1[:, 0:n], pb[:, 0:n], cos_t[:, t0:t0 + n], ALU.mult, [pk, "cos"], [("tb32", (2 * ui) % 4)])
                tt("dve", t2[:, 0:n], pr[:, 0:n], sin_t[:, t0:t0 + n], ALU.mult, [prk, "sin"], [("tb32", (2 * ui + 1) % 4)])
                tt("dve", dst_ap, t1[:, 0:n], t2[:, 0:n], ALU.add, [("tb32", (2 * ui) % 4), ("tb32", (2 * ui + 1) % 4)], [dkey])

            P.op("pool", lambda e: e.memset(UC[:, :, 0:15], 0.0), [], [("UCpad", 0)])
            P.op("pool", lambda e: e.memset(UC[:, :, 15 + EXT:UW], 0.0), [], [("UCpad", 1)])
            if h == 0 and ck("B0", []):
                return
            ui = 0
            loads = []
            for cp_ in range(8):
                loads.append((wcols(win_d, OFF_GLU + cp_ * 256, 256), wcols(win_d, OFF_GLU + D + cp_ * 256, 256)))

            def issue(i):
                (apa, sha), (apg, shg) = loads[i]
                return wload(apa, sha), wload(apg, shg)

            pend = issue(0)
            for cp_ in range(8):
                (ka, va), (kg, vg) = pend
                if cp_ + 1 < 8:
                    pend = issue(cp_ + 1)
                for j in range(2):
                    c = cp_ * 2 + j
                    for (t0, n) in tiles_ext:
                        pa, pak = proj_fm(va, ka, j, t0, n)
                        pg, pgk = proj_fm(vg, kg, j, t0, n)
                        sg = tb32[ui % 4]
                        act(sg[:, 0:n], pg[:, 0:n], AF.Sigmoid, [pgk], [("tb32", ui % 4)])
                        tt("dve", UC[:, c, 15 + t0:15 + t0 + n], pa[:, 0:n], sg[:, 0:n], ALU.mult, [pak, ("tb32", ui % 4)], [("UC", c)])
                        ui += 1
            if h == 0 and ck("B1a", []):
                return
            (kk0, vk0) = wload(*wcols(win_d, OFF_K, 256))
            (kk1, vk1) = wload(*wcols(win_d, OFF_K + 256, 256))
            (kv0, vv0) = wload(*wcols(win_d, OFF_V, 256))
            (kv1, vv1) = wload(*wcols(win_d, OFF_V + 256, 256))
            for g in range(4):
                wv_, wk_ = (vk0, kk0) if g < 2 else (vk1, kk1)
                for (t0, n) in tiles_ext:
                    pb, pk = proj_fm(wv_, wk_, g % 2, t0, n)
                    rope_evac(pb, pk, n, t0, kT[:, g, t0:t0 + n], ("kT", g), ui)
                    ui += 1
                if h == 0 and not os.environ.get("SKIP_CTXK"):
                    pb, pk = proj_fm(wv_, wk_, g % 2, 0, L, src=hcT, srck=HCK())
                    act(kc[:, g, :], pb[:, 0:L], AF.Copy, [pk], [("kc", g)])
            if h == 0 and ck("B1b", []):
                return
            nvb = NBLK + (2 if h == 0 else 0)
            for blk in range(nvb):
                pb, pk = nb()
                for hv, (wv_, wk_) in enumerate(((vv0, kv0), (vv1, kv1))):
                    for kc_ in range(16):
                        if blk < NBLK:
                            lhs = hT[:, kc_, blk * 128:(blk + 1) * 128]
                            rk = ("hT", kc_)
                        else:
                            lhs = hcT[:, kc_, (blk - NBLK) * 128:(blk - NBLK + 1) * 128]
                            rk = ("hcT", kc_)
                        mm(pb[:, hv * 256:(hv + 1) * 256], lhs, wv_[:, kc_, :], kc_ == 0, kc_ == 15, [wk_, rk], [pk], kc_ == 15)
                if blk < NBLK:
                    act(vT[:, blk, :], pb[:, :], AF.Copy, [pk], [("vT", blk)])
                else:
                    act(vc[:, blk - NBLK, :], pb[:, :], AF.Copy, [pk], [("vc", blk - NBLK)])
            if h == 0 and ck("B1", [(UC[:, 0, :], UW), (UC[:, 15, :], UW), (kT[:, 0, :], EXT), (kT[:, 3, :], EXT), (vT[:, 0, :], 512), (vT[:, 8, :], 512), (kc[:, 0, :], L), (vc[:, 1, :], 512)]):
                return
            pend = wload(*wcols(win_d, 0, 256))
            for hp in range(8):
                kq, vq = pend
                if hp + 1 < 8:
                    pend = wload(*wcols(win_d, (hp + 1) * 256, 256))
                for j in range(2):
                    hd = hp * 2 + j
                    for ti, (t0, n) in enumerate(tiles_main):
                        pb, pk = proj_fm(vq, kq, j, t0, n)
                        rope_evac(pb, pk, n, t0, QA[:, hd, ti * 512:ti * 512 + n], ("QA", hd, ti), ui)
                        ui += 1
            if h == 0 and ck("B", [(QA[:, 0, :], HALF), (QA[:, 15, :], HALF)]):
                return

            P.barrier()
            dgs = [A3[:, 0:31 * 128].rearrange("p (j m) -> p j m", j=31),
                   WR[:, 4096:4096 + 31 * 128].rearrange("p (j m) -> p j m", j=31)]
            yb32 = [f32view(A3, 4096 + i * 1024, 512) for i in range(2)]
            ysq = [A3[:, 4096 + 2048 + i * 512:4096 + 2048 + (i + 1) * 512] for i in range(2)]
            mean_bc = [f32view(A3, 8192 + i * 1024, 512) for i in range(2)]
            rstd_bc = [f32view(A3, 8192 + 2048 + i * 1024, 512) for i in range(2)]
            nmr_bc = [f32view(A3, 8192 + 4096 + i * 1024, 512) for i in range(2)]
            tC = [f32view(WR, i * 1024, 512) for i in range(4)]
            cw3 = cwf[:, :].rearrange("p (c j) -> p c j", c=16)
            u2 = 0
            for c in range(16):
                dg = dgs[c % 2]
                dgk = lambda j, c=c: ("dg", c % 2, j)
                for j in range(31):
                    if j % 2 == 0:
                        ts("dve", dg[:, j, :], ident, cw3[:, c, j:j + 1], None, ALU.mult, None, ["cst", "cwf"], [dgk(j)])
                    else:
                        act(dg[:, j, :], ident, AF.Copy, ["cst", "cwf"], [dgk(j)], scale=cw3[:, c, j:j + 1])
                pbs = []
                for ti, (t0, n) in enumerate(tiles_main):
                    pb, pk = ps[4 + (u2 % 4)], ("psC", u2 % 4)
                    u2 += 1
                    for j in range(31):
                        mm(pb[:, :], dg[:, j, :], UC[:, c, t0 + j:t0 + j + 512], j == 0, j == 30, [dgk(j), ("UC", c), ("UCpad", 0), ("UCpad", 1)], [pk], j == 30)
                    pbs.append((pb, pk, ti, t0))
                for (pb, pk, ti, t0) in pbs:
                    y32 = yb32[ti]
                    act(y32, pb[:, :], AF.Identity, [pk, "pvec"], [("y32", ti)], bias=pvec[:, c:c + 1])
                    cp("dve", UC[:, c, 15 + t0:15 + t0 + 512], y32, [("y32", ti)], [("UC", c)])
                    act(ysq[ti], y32, AF.Square, [("y32", ti)], [("ysq", ti)])
                    mm(ps[2 * ti][:, :], ones, UC[:, c, 15 + t0:15 + t0 + 512], c == 0, c == 15, ["cst", ("UC", c)], [("psS", 2 * ti)], True)
                    mm(ps[2 * ti + 1][:, :], ones, ysq[ti], c == 0, c == 15, ["cst", ("ysq", ti)], [("psS", 2 * ti + 1)], True)
            for ti, (t0, n) in enumerate(tiles_main):
                act(mean_bc[ti], ps[2 * ti][:, :], AF.Copy, [("psS", 2 * ti)], [("mean", ti)], scale=1.0 / D)
                tt("dve", tC[0], mean_bc[ti], mean_bc[ti], ALU.mult, [("mean", ti)], [("tC", 0)])
                stt("dve", tC[1], ps[2 * ti + 1][:, :], 1.0 / D, tC[0], ALU.mult, ALU.subtract, [("psS", 2 * ti + 1), ("tC", 0)], [("tC", 1)])
                act(tC[2], tC[1], AF.Sqrt, [("tC", 1)], [("tC", 2)], bias=epsb[:, 0:1])
                P.op("dve", (lambda e, ti=ti: e.reciprocal(out=rstd_bc[ti], in_=tC[2])), [("tC", 2)], [("rstd", ti)])
                stt("dve", nmr_bc[ti], mean_bc[ti], -1.0, rstd_bc[ti], ALU.mult, ALU.mult, [("mean", ti), ("rstd", ti)], [("nmr", ti)])
            u3 = 0
            for c in range(16):
                for ti, (t0, n) in enumerate(tiles_main):
                    ta = tC[u3 % 2]
                    tb_ = tC[2 + (u3 % 2)]
                    sl = UC[:, c, 15 + t0:15 + t0 + 512]
                    tt("dve", ta, sl, rstd_bc[ti], ALU.mult, [("UC", c), ("rstd", ti)], [("tC", u3 % 2)])
                    tt("pool", tb_, ta, nmr_bc[ti], ALU.add, [("tC", u3 % 2), ("nmr", ti)], [("tC", 2 + (u3 % 2))])
                    act(sl, tb_, AF.Silu, [("tC", 2 + (u3 % 2)), "pvec"], [("UC", c)], scale=pvec[:, 16 + c:17 + c], bias=pvec[:, 32 + c:33 + c])
                    u3 += 1
            if h == 0 and ck("C", [(UC[:, 0, :], UW), (UC[:, 15, :], UW)]):
                return

            P.barrier()
            pTb = [A3[:, i * 512:(i + 1) * 512] for i in range(6)]
            dsb = [f32view(A3, 3072 + i * 1024, 512) for i in range(2)]
            u4 = 0
            pj = 0
            for qb in range(8):
                lb = (m0 // 128) + qb
                gb = (gm0 // 128) + qb
                kbs = []
                if gb - 1 >= 0:
                    kbs.append(("w", lb - 1, maskp))
                kbs.append(("w", lb, None))
                if gb + 1 < 16:
                    kbs.append(("w", lb + 1, maskn))
                kbs.append(("c", 0, None))
                kbs.append(("c", 1, None))
                for g in range(4):
                    qap = QA[:, 4 * g:4 * g + 4, qb * 128:(qb + 1) * 128]
                    qkeys = [("QA", 4 * g + hh, qb // 4) for hh in range(4)]
                    ob, obk = ps[4 + 2 * (u4 % 2)], ("psO", 2 * (u4 % 2))
                    db, dbk = ps[5 + 2 * (u4 % 2)], ("psO", 2 * (u4 % 2) + 1)
                    pts = []
                    for (kind, kb, msk) in kbs:
                        sb_, sbk = ps[pj % 4], ("psQ", pj % 4)
                        if kind == "w":
                            lhs = kT[:, g, kb * 128:(kb + 1) * 128]
                            lk = ("kT", g)
                        else:
                            lhs = kc[:, g, kb * 128:(kb + 1) * 128]
                            lk = ("kc", g)
                        mm(sb_[:, :], lhs, qap, True, msk is None, [lk] + qkeys, [sbk], msk is None)
                        if msk is not None:
                            mm(sb_[:, :], ident, msk, False, True, ["cst"], [sbk], True)
                        pt = pTb[pj % 6]
                        act(pt, sb_[:, :], AF.Exp, [sbk], [("pT", pj % 6)], scale=SCALE)
                        pts.append((kind, kb, pt, ("pT", pj % 6)))
                        pj += 1
                    for i, (kind, kb, pt, ptk) in enumerate(pts):
                        if kind == "w":
                            lhs = vT[:, kb, g * 128:(g + 1) * 128]
                            lk = ("vT", kb)
                        else:
                            lhs = vc[:, kb, g * 128:(g + 1) * 128]
                            lk = ("vc", kb)
                        last = i == len(pts) - 1
                        mm(ob[:, :], lhs, pt, i == 0, last, [lk, ptk], [obk], last)
                        mm(db[:, :], ones, pt, i == 0, last, ["cst", ptk], [dbk], last)
                    ds_ = dsb[u4 % 2]
                    for hh in range(4):
                        ts("dve", ds_[:, hh * 128:(hh + 1) * 128], db[:, hh * 128:(hh + 1) * 128], esink[:, 4 * g + hh:4 * g + hh + 1], None,
                           ALU.add, None, [dbk, "esink"], [("ds", u4 % 2, hh)])
                    P.op("dve", (lambda e, ds_=ds_: e.reciprocal(out=ds_, in_=ds_)), [("ds", u4 % 2, hh) for hh in range(4)], [("dsr", u4 % 2)])
                    tt("dve", qap, ob[:, :].rearrange("p (h q) -> p h q", h=4), ds_.rearrange("p (h q) -> p h q", h=4), ALU.mult,
                       [obk, ("dsr", u4 % 2)], [("AO", 4 * g + hh, qb) for hh in range(4)] + qkeys)
                    u4 += 1
            AOK = lambda c: [("AO", c, qb) for qb in range(8)]
            if h == 0 and ck("D", [(QA[:, 0, :], HALF), (QA[:, 15, :], HALF)]):
                return

            P.barrier()
            e32 = [f32view(A4, i * 1024, 512) for i in range(8)]
            u5 = 0

            def issueE(f):
                return (wload(*wcols(wap_d, f * 128, 128)), wload(*wcols(wcp_d, f * 128, 128)),
                        wload(*wcols(win_d, OFF_GATE + f * 128, 128)), wload(*wcols(win_d, OFF_GATE + D + f * 128, 128)))

            pendE = issueE(0)
            for f in range(16):
                ws = pendE
                if f + 1 < 16:
                    pendE = issueE(f + 1)
                j = 0
                for ti, (t0, n) in enumerate(tiles_main):
                    (ka, va), (kcv, vcv), (kga, vga), (kgb, vgb) = ws
                    pa, pak = nb()
                    for kc_ in range(16):
                        mm(pa[:, :], va[:, kc_, 0:128], QA[:, kc_, ti * 512:(ti + 1) * 512], kc_ == 0, kc_ == 15,
                           [ka] + AOK(kc_), [pak], kc_ == 15)
                    pc, pck = nb()
                    for kc_ in range(16):
                        mm(pc[:, :], vcv[:, kc_, 0:128], UC[:, kc_, 15 + t0:15 + t0 + 512], kc_ == 0, kc_ == 15,
                           [kcv, ("UC", kc_)], [pck], kc_ == 15)
                    pga, pgak = proj_fm(vga, kga, 0, t0, 512)
                    pgb, pgbk = proj_fm(vgb, kgb, 0, t0, 512)
                    sA = e32[(4 * u5) % 8]
                    sB = e32[(4 * u5 + 1) % 8]
                    t1 = e32[(4 * u5 + 2) % 8]
                    t2 = e32[(4 * u5 + 3) % 8]
                    kA, kB, k1, k2 = [("e32", (4 * u5 + i) % 8) for i in range(4)]
                    act(sA, pga[:, :], AF.Sigmoid, [pgak], [kA])
                    act(sB, pgb[:, :], AF.Sigmoid, [pgbk], [kB])
                    tt("dve", t1, pa[:, :], sA, ALU.mult, [pak, kA], [k1])
                    tt("dve", t2, pc[:, :], sB, ALU.mult, [pck, kB], [k2])
                    tt("pool", MG[:, f, ti * 512:(ti + 1) * 512], t1, t2, ALU.add, [k1, k2], [("MG", f)])
                    u5 += 1
            if h == 0 and ck("E", [(MG[:, 0, :], HALF), (MG[:, 15, :], HALF)]):
                return

            P.barrier()
            xr = [f32view(A2, i * 2 * D, D) for i in range(2)]
            gbc = [f32view(A2, 4 * D + i * 1024, 512) for i in range(2)]
            f32t = [f32view(A2, 4 * D + 2048 + i * 1024, 512) for i in range(4)]
            lnb = [f32view(A2, i * 2 * D, D) for i in range(2)]
            for tb in range(8):
                i = tb % 2
                dma("sp", xr[i], x_d[gm0 + tb * 128:gm0 + (tb + 1) * 128, :], writes=[("xr", i)])
                lbk = m0 // 128 + tb
                ts("dve", sa8[:, 2 * tb:2 * tb + 1], XS(lbk, 0), ALPHA, None, ALU.mult, None, [], [("sa", tb, 0)])
                ts("dve", sa8[:, 2 * tb + 1:2 * tb + 2], XS(lbk, 1), ALPHA, None, ALU.mult, None, [], [("sa", tb, 1)])
                act(PL[:, tb, :], xr[i], AF.Identity, [("xr", i), ("sa", tb, 0), ("sa", tb, 1)], [("PL", tb)], scale=sa8[:, 2 * tb:2 * tb + 1], bias=sa8[:, 2 * tb + 1:2 * tb + 2])
            u6 = 0
            for cg in range(4):
                (k0, v0) = wload(*wcols(wout_d, cg * 512, 256))
                (k1_, v1_) = wload(*wcols(wout_d, cg * 512 + 256, 256))
                gb_ = gbc[cg % 2]
                dma("sp", gb_, gsc_d[0:1, cg * 512:(cg + 1) * 512].partition_broadcast(128), reads=[("gsc", 0, cg)], writes=[("gbc", cg % 2)])
                for tb in range(8):
                    pb, pk = nb()
                    for hv, (wv_, wk_) in enumerate(((v0, k0), (v1_, k1_))):
                        for kc_ in range(16):
                            mm(pb[:, hv * 256:(hv + 1) * 256], MG[:, kc_, tb * 128:(tb + 1) * 128], wv_[:, kc_, :], kc_ == 0, kc_ == 15,
                               [wk_, ("MG", kc_)], [pk], kc_ == 15)
                    t_ = f32t[u6 % 4]
                    tt("dve", t_, pb[:, :], gb_, ALU.mult, [pk, ("gbc", cg % 2)], [("f32t", u6 % 4)])
                    tt("dve", PL[:, tb, cg * 512:(cg + 1) * 512], PL[:, tb, cg * 512:(cg + 1) * 512], t_, ALU.add, [("f32t", u6 % 4), ("PL", tb)], [("PL", tb)])
                    u6 += 1
            P.barrier()
            h2T = A3[:, :].rearrange("p (c t) -> p c t", c=16)
            dma("sp", lnb[0], lnrow_d[0:1, :].partition_broadcast(128), writes=[("lnb", 0)])
            dma("sp", lnb[1], lnrow_d[1:2, :].partition_broadcast(128), writes=[("lnb", 1)])
            act(lnb[0], lnb[0], AF.Copy, [("lnb", 0)], [("lnb", 0)], scale=ALPHA)
            act(lnb[1], lnb[1], AF.Copy, [("lnb", 1)], [("lnb", 1)], scale=ALPHA)
            xb2 = [A2[:, 4 * D + i * D:4 * D + (i + 1) * D] for i in range(2)]
            for tb in range(8):
                i = tb % 2
                plt = PL[:, tb, :]
                ln_stats(plt, ("PL", tb), stat[:, 42:43], stat[:, 43:44], ("s1", 0))
                act(plt, plt, AF.Identity, [("PL", tb), ("s1", 0)], [("PL", tb)], scale=stat[:, 42:43], bias=stat[:, 43:44])
                tt("dve", plt, plt, lnb[0], ALU.mult, [("PL", tb), ("lnb", 0)], [("PL", tb)])
                tt("dve", plt, plt, lnb[1], ALU.add, [("PL", tb), ("lnb", 1)], [("PL", tb)])
                cp("act", xb2[i], plt, [("PL", tb)], [("xb2", i)])
                for hb in range(2):
                    pb = ps[(2 * tb + hb) % 4]
                    pbb = pb[:, :].bitcast(BF16)
                    pk = ("psA", (2 * tb + hb) % 4)
                    for j in range(8):
                        c = hb * 8 + j
                        tr(pbb[:, j * 128:(j + 1) * 128], xb2[i][:, c * 128:(c + 1) * 128], [("xb2", i), "cst"], [pk], j == 7)
                    for j in range(8):
                        c = hb * 8 + j
                        act(h2T[:, c, tb * 128:(tb + 1) * 128], pbb[:, j * 128:(j + 1) * 128], AF.Identity,
                            [pk, "modT", "sp1"], [("h2T", c)], scale=SC2(c), bias=SH2(c))
            if h == 0 and ck("F", [(PL[:, 0, :], D), (PL[:, 7, :], D), (h2T[:, 0, :], HALF), (h2T[:, 15, :], HALF)]):
                return

            P.barrier()
            g2f = f32view(A2, 0, D)
            dma("sp", g2f, gsc_d[1:2, :].partition_broadcast(128), reads=[("gsc", 1, q) for q in range(4)], writes=["g2f"])
            hid = [A2[:, 4096 + i * 4096:4096 + (i + 1) * 4096].rearrange("p (c t) -> p c t", c=4) for i in range(2)]
            gs32 = [f32view(A2, 12288 + i * 1024, 512) for i in range(2)]
            ft = [f32view(A2, 14336 + i * 1024, 512) for i in range(4)]
            NG = HID // 512
            u7 = 0
            u8 = 0

            def issueG(gi, sub):
                c0 = gi * 512 + sub * 256
                return (wload(*wcols(wffi_d, c0, 256)), wload(*wcols(wffi_d, HID + c0, 256)))

            def ffn_in(sub, wpair, hb_, gi):
                nonlocal u7
                (kg_, vg_), (ku_, vu_) = wpair
                for j in range(2):
                    jj = sub * 2 + j
                    for ti in range(2):
                        pg, pgk = nb()
                        for kc_ in range(16):
                            mm(pg[:, :], vg_[:, kc_, j * 128:(j + 1) * 128], h2T[:, kc_, ti * 512:(ti + 1) * 512], kc_ == 0, kc_ == 15, [kg_, ("h2T", kc_)], [pgk], kc_ == 15)
                        pu, puk = nb()
                        for kc_ in range(16):
                            mm(pu[:, :], vu_[:, kc_, j * 128:(j + 1) * 128], h2T[:, kc_, ti * 512:(ti + 1) * 512], kc_ == 0, kc_ == 15, [ku_, ("h2T", kc_)], [puk], kc_ == 15)
                        s_ = gs32[u7 % 2]
                        act(s_, pg[:, :], AF.Silu, [pgk], [("gs32", u7 % 2)])
                        tt("dve", hb_[:, jj, ti * 512:(ti + 1) * 512], pu[:, :], s_, ALU.mult, [puk, ("gs32", u7 % 2)], [("hid", gi % 2, jj)])
                        u7 += 1

            def wout_rows(gi, sub):
                r0 = gi * 512 + sub * 256
                return wload(wffo_d[r0:r0 + 256, :].rearrange("(c p) n -> p c n", p=128), (2, D))

            in_a = issueG(0, 0)
            for gi in range(NG):
                hb_ = hid[gi % 2]
                in_b = issueG(gi, 1)
                ffn_in(0, in_a, hb_, gi)
                outs_w = [wout_rows(gi, 0), wout_rows(gi, 1)]
                ffn_in(1, in_b, hb_, gi)
                if gi + 1 < NG:
                    in_a = issueG(gi + 1, 0)
                for tb in range(8):
                    for cg in range(4):
                        pb, pk = nb()
                        for jj in range(4):
                            ko, vo = outs_w[jj // 2]
                            mm(pb[:, :], hb_[:, jj, tb * 128:(tb + 1) * 128], vo[:, jj % 2, cg * 512:(cg + 1) * 512], jj == 0, jj == 3,
                               [ko, ("hid", gi % 2, jj)], [pk], jj == 3)
                        t_ = ft[u8 % 4]
                        tt("dve", t_, pb[:, :], g2f[:, cg * 512:(cg + 1) * 512], ALU.mult, [pk, "g2f"], [("ft", u8 % 4)])
                        tt("pool", PL[:, tb, cg * 512:(cg + 1) * 512], PL[:, tb, cg * 512:(cg + 1) * 512], t_, ALU.add, [("ft", u8 % 4), ("PL", tb)], [("PL", tb)])
                        u8 += 1
            P.barrier()
            l2 = [f32view(A2, 0, D), f32view(A2, 2 * D, D)]
            dma("sp", l2[0], lnrow_d[2:3, :].partition_broadcast(128), writes=[("l2", 0)])
            dma("sp", l2[1], lnrow_d[3:4, :].partition_broadcast(128), writes=[("l2", 1)])
            for tb in range(8):
                plt = PL[:, tb, :]
                ln_stats(plt, ("PL", tb), stat[:, 44:45], stat[:, 45:46], ("s2", 0))
                act(plt, plt, AF.Identity, [("PL", tb), ("s2", 0)], [("PL", tb)], scale=stat[:, 44:45], bias=stat[:, 45:46])
                tt("dve", plt, plt, l2[0], ALU.mult, [("PL", tb), ("l2", 0)], [("PL", tb)])
                tt("dve", plt, plt, l2[1], ALU.add, [("PL", tb), ("l2", 1)], [("PL", tb)])
                dma("sp", out_d[gm0 + tb * 128:gm0 + (tb + 1) * 128, :], plt, reads=[("PL", tb)], writes=[("out", h, tb)])
            P.barrier()

        for h in range(2):
            emit_half(h)
            if state["dbg_done"]:
                break
        if not state["dbg_done"]:
            P.barrier()
        P.emit(sems, block)
    return nc


def _consts():
    ident = np.eye(128, dtype=np.float32)
    ones = np.ones((128, 128), np.float32)
    perm = np.zeros((128, 128), np.float32)
    for m in range(128):
        half = (m // 32) % 2
        if half == 0:
            perm[m + 32, m] = -1.0
        else:
            perm[m - 32, m] = 1.0
    j = np.arange(128)[:, None]
    q = np.arange(128)[None, :]
    mp = np.where(j >= q, 0.0, -30000.0).astype(np.float32)
    mn = np.where(j <= q, 0.0, -30000.0).astype(np.float32)
    maskp = np.tile(mp, (1, 4))
    maskn = np.tile(mn, (1, 4))
    cst = np.concatenate([ident, ones, perm, maskp, maskn], axis=1).astype(np.float32)
    t = np.arange(S)
    row = (t // 64).astype(np.float32)
    col = (t % 64).astype(np.float32)
    inv = (10000.0 ** (-np.arange(32, dtype=np.float32) / 32.0)).astype(np.float32)
    ang = np.zeros((128, S), np.float32)
    for p in range(128):
        axis = p // 64
        f = p % 32
        pos = row if axis == 0 else col
        ang[p] = pos * inv[f]
    rope = np.stack([np.cos(ang), np.sin(ang)]).astype(np.float32)
    return cst, rope


def _pc(v):
    return np.ascontiguousarray(v.reshape(16, 128).T)


def make_in_maps(x, c, ctx, c_ctx, w_mod, b_mod, w_in, attn_sink, conv_w, conv_b, conv_norm_g, conv_norm_b,
                 w_attn_proj, w_conv_proj, w_out, ln1_g, ln1_b, w_ffn_in, w_ffn_out, ln2_g, ln2_b):
    f = lambda a: np.ascontiguousarray(np.asarray(a, dtype=np.float32))
    cst, rope = _consts()
    bm = f(b_mod[0])
    shared = {
        "wmod": f(w_mod[0]),
        "bmodt": np.ascontiguousarray(bm.reshape(96, 128).T),
        "bmodrow": bm.reshape(1, -1),
        "win": f(w_in[0]),
        "sinkbc": np.ascontiguousarray(np.broadcast_to(f(attn_sink[0])[None, :], (128, NH))),
        "convwt": np.ascontiguousarray(f(conv_w[0]).T.reshape(16, 128, 31).transpose(1, 0, 2).reshape(128, 16 * 31)),
        "pvec": np.concatenate([_pc(f(conv_b[0])), _pc(f(conv_norm_g[0])), _pc(f(conv_norm_b[0]))], axis=1),
        "wap": f(w_attn_proj[0]),
        "wcp": f(w_conv_proj[0]),
        "wout": f(w_out[0]),
        "lnrow": np.stack([f(ln1_g[0]), f(ln1_b[0]), f(ln2_g[0]), f(ln2_b[0])]),
        "wffi": f(w_ffn_in[0]),
        "wffo": f(w_ffn_out[0]),
        "cst": cst,
        "rope": rope,
    }
    maps = []
    xs = f(x)
    cs = f(c)
    cx = f(ctx)
    cc = _pc(f(c_ctx))
    for b in range(8):
        m = dict(shared)
        m["xin"] = xs[b]
        m["ctxin"] = cx[b]
        m["cvec"] = np.ascontiguousarray(np.concatenate([_pc(cs[b]), cc], axis=1))
        maps.append(m)
    return maps


_NC_CACHE = {}


def kernel(**inputs):
    if "nc" not in _NC_CACHE:
        _NC_CACHE["nc"] = build_program()
    nc = _NC_CACHE["nc"]
    maps = make_in_maps(**inputs)
    res = run_bass_kernel_spmd(nc, maps, core_ids=list(range(8)))
    return np.stack([np.asarray(r["yout"], dtype=np.float32) for r in res.results], axis=0)
```
